# Optimizing a Trainium2 kernel written in Bass

```python
import math
import jax, jax.numpy as jnp
from jax import lax
import numpy as np

D_MODEL = 2048
BATCH = 2
SEQ = 16384
DEPTH = 2

CHUNK = 64
EPS = 1e-6
N_MOD = 9
D_FF = 5632
FFN_RES = 0.5

SSD_WIDTH = D_MODEL
SSD_HEAD_DIM = 64
SSD_HEADS = SSD_WIDTH // SSD_HEAD_DIM
SSD_GROUPS = 4
SSD_STATE = 128
CONV_WIDTH = 4
SSD_CONV_CH = SSD_WIDTH + 2 * SSD_GROUPS * SSD_STATE
DT_MIN = 1e-3
DT_MAX = 1e-1

RET_WIDTH = D_MODEL
RET_HEADS = 8
RET_HEAD_DIM = RET_WIDTH // RET_HEADS
ROPE_BASE = 10000.0

MIX_WIDTH = SSD_WIDTH + RET_WIDTH
SPLIT_POINTS = (
    SSD_WIDTH,
    SSD_WIDTH + SSD_CONV_CH,
    SSD_WIDTH + SSD_CONV_CH + SSD_HEADS,
    SSD_WIDTH + SSD_CONV_CH + SSD_HEADS + RET_WIDTH,
    SSD_WIDTH + SSD_CONV_CH + SSD_HEADS + 2 * RET_WIDTH,
    SSD_WIDTH + SSD_CONV_CH + SSD_HEADS + 3 * RET_WIDTH,
)
IN_COLS = SSD_WIDTH + SSD_CONV_CH + SSD_HEADS + 4 * RET_WIDTH

kernel_name = "hybrid_ssd_retention_macaron_adaln"


def rmsnorm(x, w):
    xf = x.astype(jnp.float32)
    y = xf * lax.rsqrt(jnp.mean(xf * xf, axis=-1, keepdims=True) + EPS)
    return (y * w.astype(jnp.float32)).astype(x.dtype)


def modulate(x, shift, scale):
    return x * (1.0 + scale[:, None, :]) + shift[:, None, :]


def swiglu(x, w1, w3, w2):
    return (jax.nn.silu(x @ w1) * (x @ w3)) @ w2


def causal_dwconv(u, w, b):
    out = lax.conv_general_dilated(
        u, w[:, None, :].astype(u.dtype), window_strides=(1,), padding=[(w.shape[0] - 1, 0)],
        dimension_numbers=("NWC", "WIO", "NWC"), feature_group_count=u.shape[-1])
    return out + b


def ssd_mixer(z, xbc, dt_raw, conv_w, conv_b, dt_bias, a_log, d_skip, norm_w):
    f32 = jnp.float32
    bsz, seq, _ = z.shape
    nc = seq // CHUNK
    G, R, P, N = SSD_GROUPS, SSD_HEADS // SSD_GROUPS, SSD_HEAD_DIM, SSD_STATE
    xbc = jax.nn.silu(causal_dwconv(xbc, conv_w, conv_b))
    xs, bm, cm = jnp.split(xbc, [SSD_WIDTH, SSD_WIDTH + G * N], axis=-1)
    xs = xs.astype(f32)
    bm = bm.astype(f32).reshape(bsz, nc, CHUNK, G, N)
    cm = cm.astype(f32).reshape(bsz, nc, CHUNK, G, N)
    dt = jax.nn.softplus(dt_raw.astype(f32) + dt_bias.astype(f32))
    a = -jnp.exp(a_log.astype(f32))
    d_a = jnp.moveaxis((dt * a).reshape(bsz, nc, CHUNK, G, R), 2, -1)
    a_cum = jnp.cumsum(d_a, axis=-1)
    xdt = xs.reshape(bsz, nc, CHUNK, G, R, P) * dt.reshape(bsz, nc, CHUNK, G, R)[..., None]
    causal = jnp.tril(jnp.ones((CHUNK, CHUNK), dtype=bool))
    seg = a_cum[..., :, None] - a_cum[..., None, :]
    l_mat = jnp.exp(jnp.where(causal, seg, -jnp.inf))
    cb = jnp.einsum("bclgn,bcsgn->bcgls", cm, bm)
    y_diag = jnp.einsum("bcgls,bcgrls,bcsgrp->bclgrp", cb, l_mat, xdt)
    decay_in = jnp.exp(a_cum)
    decay_out = jnp.exp(a_cum[..., -1:] - a_cum)
    chunk_decay = jnp.exp(a_cum[..., -1])

    def step(state, inp):
        c_c, b_c, x_c, d_in, d_out, d_ch = inp
        y_off = jnp.einsum("blgn,bgrpn,bgrl->blgrp", c_c, state, d_in)
        state = state * d_ch[..., None, None] + jnp.einsum("bsgn,bgrs,bsgrp->bgrpn", b_c, d_out, x_c)
        return state, y_off

    scan_in = tuple(jnp.moveaxis(t, 1, 0) for t in (cm, bm, xdt, decay_in, decay_out, chunk_decay))
    state0 = jnp.zeros((bsz, G, R, P, N), f32)
    _, y_off = lax.scan(step, state0, scan_in)
    y = (y_diag + jnp.moveaxis(y_off, 0, 1)).reshape(bsz, seq, SSD_HEADS, P)
    y = y + d_skip.astype(f32)[:, None] * xs.reshape(bsz, seq, SSD_HEADS, P)
    y = y.reshape(bsz, seq, SSD_WIDTH) * jax.nn.silu(z.astype(f32))
    yg = y.reshape(bsz, seq, G, SSD_WIDTH // G)
    yg = yg * lax.rsqrt(jnp.mean(yg * yg, axis=-1, keepdims=True) + EPS)
    y = yg.reshape(bsz, seq, SSD_WIDTH) * norm_w.astype(f32)
    return y.astype(z.dtype)


def rotary(t, positions):
    half = t.shape[-1] // 2
    inv_freq = ROPE_BASE ** (-jnp.linspace(0.0, 1.0, half, dtype=jnp.float32))
    ang = positions.astype(jnp.float32)[:, :, None] * inv_freq
    cos = jnp.cos(ang)[:, :, None, :]
    sin = jnp.sin(ang)[:, :, None, :]
    t1, t2 = t[..., :half], t[..., half:]
    return jnp.concatenate([t1 * cos - t2 * sin, t1 * sin + t2 * cos], axis=-1)


def retention_mixer(q, k, v, g, positions, norm_w):
    f32 = jnp.float32
    bsz, seq, _ = q.shape
    nc = seq // CHUNK
    H, Dh = RET_HEADS, RET_HEAD_DIM
    q = rotary(q.astype(f32).reshape(bsz, seq, H, Dh), positions)
    k = rotary(k.astype(f32).reshape(bsz, seq, H, Dh), positions) * (Dh ** -0.5)
    v = v.astype(f32).reshape(bsz, seq, H, Dh)
    log_gamma = jnp.log1p(-jnp.exp2(-5.0 - jnp.arange(H, dtype=f32)))
    idx = jnp.arange(CHUNK, dtype=f32)
    intra_decay = jnp.exp(log_gamma[:, None, None] * jnp.abs(idx[:, None] - idx[None, :]))
    q_decay = jnp.exp(log_gamma[:, None] * (idx + 1.0))
    k_decay = jnp.exp(log_gamma[:, None] * (CHUNK - 1.0 - idx))
    chunk_decay = jnp.exp(log_gamma * CHUNK)
    qc = q.reshape(bsz, nc, CHUNK, H, Dh)
    kc = k.reshape(bsz, nc, CHUNK, H, Dh)
    vc = v.reshape(bsz, nc, CHUNK, H, Dh)
    scores = jnp.einsum("bclhd,bcshd->bchls", qc, kc) * intra_decay
    y_intra = jnp.einsum("bchls,bcshe->bclhe", scores, vc)

    def step(state, inp):
        q_c, k_c, v_c = inp
        y_cross = jnp.einsum("blhd,bhde,hl->blhe", q_c, state, q_decay)
        state = state * chunk_decay[:, None, None] + jnp.einsum("bshd,hs,bshe->bhde", k_c, k_decay, v_c)
        return state, y_cross

    state0 = jnp.zeros((bsz, H, Dh, Dh), f32)
    _, y_cross = lax.scan(step, state0, (jnp.moveaxis(qc, 1, 0), jnp.moveaxis(kc, 1, 0), jnp.moveaxis(vc, 1, 0)))
    y = (y_intra + jnp.moveaxis(y_cross, 0, 1)).reshape(bsz, seq, H, Dh)
    y = y * lax.rsqrt(jnp.mean(y * y, axis=-1, keepdims=True) + EPS)
    y = y.reshape(bsz, seq, RET_WIDTH) * norm_w.astype(f32)
    y = jax.nn.silu(g.astype(f32)) * y
    return y.astype(g.dtype)


def setup_inputs(seed: int = 0) -> dict:
    key = jax.random.key(seed)
    ks = jax.random.split(key, 24)
    f32 = jnp.float32
    L, D = DEPTH, D_MODEL

    def nrm(k, shape, scale):
        return jax.random.normal(k, shape, f32) * scale

    def gain(k, shape):
        return 1.0 + 0.02 * jax.random.normal(k, shape, f32)

    x = nrm(ks[0], (BATCH, SEQ, D), 1.0)
    c = nrm(ks[1], (BATCH, D), 1.0)
    offset = jax.random.randint(ks[2], (BATCH, 1), 0, 4096, dtype=jnp.int32)
    positions = offset + jnp.arange(SEQ, dtype=jnp.int32)[None, :]
    ada_w = nrm(ks[3], (L, D, N_MOD * D), 0.5 * D ** -0.5)
    ada_b = nrm(ks[4], (L, N_MOD * D), 0.02)
    norm_ffn1_w = gain(ks[5], (L, D))
    ffn1_w1 = nrm(ks[6], (L, D, D_FF), D ** -0.5)
    ffn1_w3 = nrm(ks[7], (L, D, D_FF), D ** -0.5)
    ffn1_w2 = nrm(ks[8], (L, D_FF, D), D_FF ** -0.5)
    norm_mix_w = gain(ks[9], (L, D))
    w_in = nrm(ks[10], (L, D, IN_COLS), D ** -0.5)
    conv_w = nrm(ks[11], (L, CONV_WIDTH, SSD_CONV_CH), CONV_WIDTH ** -0.5)
    conv_b = nrm(ks[12], (L, SSD_CONV_CH), 0.02)
    dt0 = jnp.exp(jax.random.uniform(ks[13], (L, SSD_HEADS), f32, math.log(DT_MIN), math.log(DT_MAX)))
    dt_bias = dt0 + jnp.log(-jnp.expm1(-dt0))
    a_log = jnp.log(jax.random.uniform(ks[14], (L, SSD_HEADS), f32, 1.0, 16.0))
    d_skip = 1.0 + 0.1 * jax.random.normal(ks[15], (L, SSD_HEADS), f32)
    ssd_norm_w = gain(ks[16], (L, SSD_WIDTH))
    ret_norm_w = gain(ks[17], (L, RET_WIDTH))
    w_out = nrm(ks[18], (L, MIX_WIDTH, D), MIX_WIDTH ** -0.5)
    norm_ffn2_w = gain(ks[19], (L, D))
    ffn2_w1 = nrm(ks[20], (L, D, D_FF), D ** -0.5)
    ffn2_w3 = nrm(ks[21], (L, D, D_FF), D ** -0.5)
    ffn2_w2 = nrm(ks[22], (L, D_FF, D), D_FF ** -0.5)
    final_norm_w = gain(ks[23], (D,))
    return {
        "x": x, "c": c, "positions": positions,
        "ada_w": ada_w, "ada_b": ada_b,
        "norm_ffn1_w": norm_ffn1_w, "ffn1_w1": ffn1_w1, "ffn1_w3": ffn1_w3, "ffn1_w2": ffn1_w2,
        "norm_mix_w": norm_mix_w, "w_in": w_in, "conv_w": conv_w, "conv_b": conv_b,
        "dt_bias": dt_bias, "a_log": a_log, "d_skip": d_skip,
        "ssd_norm_w": ssd_norm_w, "ret_norm_w": ret_norm_w, "w_out": w_out,
        "norm_ffn2_w": norm_ffn2_w, "ffn2_w1": ffn2_w1, "ffn2_w3": ffn2_w3, "ffn2_w2": ffn2_w2,
        "final_norm_w": final_norm_w,
    }


def reference(x, c, positions, ada_w, ada_b, norm_ffn1_w, ffn1_w1, ffn1_w3, ffn1_w2,
              norm_mix_w, w_in, conv_w, conv_b, dt_bias, a_log, d_skip,
              ssd_norm_w, ret_norm_w, w_out, norm_ffn2_w, ffn2_w1, ffn2_w3, ffn2_w2,
              final_norm_w):
    h = x
    cond = jax.nn.silu(c)
    for i in range(DEPTH):
        mod = cond @ ada_w[i] + ada_b[i]
        sh1, sc1, g1, sh2, sc2, g2, sh3, sc3, g3 = jnp.split(mod, N_MOD, axis=-1)
        n = modulate(rmsnorm(h, norm_ffn1_w[i]), sh1, sc1)
        h = h + FFN_RES * g1[:, None, :] * swiglu(n, ffn1_w1[i], ffn1_w3[i], ffn1_w2[i])
        n = modulate(rmsnorm(h, norm_mix_w[i]), sh2, sc2)
        proj = n @ w_in[i]
        z, xbc, dt_raw, q, k, v, g = jnp.split(proj, SPLIT_POINTS, axis=-1)
        y_ssd = ssd_mixer(z, xbc, dt_raw, conv_w[i], conv_b[i], dt_bias[i], a_log[i], d_skip[i], ssd_norm_w[i])
        y_ret = retention_mixer(q, k, v, g, positions, ret_norm_w[i])
        mix = jnp.concatenate([y_ssd, y_ret], axis=-1) @ w_out[i]
        h = h + g2[:, None, :] * mix
        n = modulate(rmsnorm(h, norm_ffn2_w[i]), sh3, sc3)
        h = h + FFN_RES * g3[:, None, :] * swiglu(n, ffn2_w1[i], ffn2_w3[i], ffn2_w2[i])
    return rmsnorm(h, final_norm_w)
```

```python
import math
import os
import contextlib
import numpy as np
import concourse.bass as bass
import concourse.mybir as mybir
from concourse.bass_utils import run_bass_kernel_spmd

F32 = mybir.dt.float32
BF16 = mybir.dt.bfloat16
I32 = mybir.dt.int32
AF = mybir.ActivationFunctionType
ALU = mybir.AluOpType

D = 2048
DFF = 5632
L = 2
T = 512
EPS = 1e-6
NPV = 672
ZO, XO, BO, CO, DTO, QO, KO, VO, GO = 0, 2048, 4096, 4608, 5120, 5152, 7200, 9248, 11296
P_N1, P_N2, P_N3, P_ADAB, P_RNW, P_SNW, P_CW, P_CB, P_FN, P_C, P_DTB, P_ALOG, P_DSK = (
    0, 16, 32, 48, 192, 208, 224, 320, 344, 360, 376, 504, 632)
C_ID, C_TRI, C_ONE, C_NEG, C_IOTA, C_IFQ, C_DFIX = 0, 128, 256, 384, 896, 1408, 1409
NCST = 1409 + 8 * 128


class Tok:
    __slots__ = ("w", "r")

    def __init__(self):
        self.w = None
        self.r = []


class B:
    __slots__ = ("ap", "t")

    def __init__(self, ap, t=None):
        self.ap = ap
        self.t = t if t is not None else Tok()


class _Rec:
    def __getattr__(self, name):
        return lambda *a, **k: (name, a, k)


def _eager(fn):
    name, a, k = fn(_Rec())
    return lambda e: getattr(e, name)(*a, **k)


class Sched:
    def __init__(self, nc, stack):
        self.nc = nc
        self.stack = stack
        self.engs = ["pe", "act", "dve", "pool", "sp"]
        self.sems = {}
        self.cnt = {}
        self.unit = {}
        self.waited = {e: {} for e in self.engs}
        self.streams = {e: [] for e in self.engs}
        self.pe_chain = False
        for e in self.engs:
            self.newsem(e, 1)

    def newsem(self, name, unit):
        self.sems[name] = self.stack.enter_context(self.nc.semaphore("s_" + name))
        self.cnt[name] = 0
        self.unit[name] = unit
        return name

    def _deps(self, eng, reads, writes):
        deps = {}
        for t in reads:
            if t.w is not None:
                deps[t.w[0]] = max(deps.get(t.w[0], 0), t.w[1])
        for t in writes:
            if t.w is not None:
                deps[t.w[0]] = max(deps.get(t.w[0], 0), t.w[1])
            for (e2, c) in t.r:
                deps[e2] = max(deps.get(e2, 0), c)
        waits = []
        for e2, c in deps.items():
            if e2 == "pe" and eng == "pe" and self.pe_chain:
                continue
            if self.unit[e2] == 16:
                c = self.cnt[e2]
            if self.waited[eng].get(e2, 0) < c:
                self.waited[eng][e2] = c
                waits.append((e2, c * self.unit[e2]))
        return waits

    def _mark(self, me, reads, writes):
        for t in reads:
            t.r = [x for x in t.r if x[0] != me[0]] + [me]
        for t in writes:
            t.w = me
            t.r = []

    def op(self, eng, fn, reads=(), writes=()):
        waits = self._deps(eng, reads, writes)
        self.cnt[eng] += 1
        self.streams[eng].append((waits, _eager(fn), eng, 1))
        self._mark((eng, self.cnt[eng]), reads, writes)

    def dma(self, q, dsem, fn, reads=(), writes=()):
        waits = self._deps(q, reads, writes)
        self.cnt[dsem] += 1
        self.streams[q].append((waits, _eager(fn), dsem, 16))
        self._mark((dsem, self.cnt[dsem]), reads, writes)

    def barrier(self):
        for e in self.engs:
            waits = []
            for n in self.sems:
                c = self.cnt[n]
                if c > 0 and self.waited[e].get(n, 0) < c:
                    self.waited[e][n] = c
                    waits.append((n, c * self.unit[n]))
            if waits:
                self.streams[e].append((waits, None, None, 0))

    def emit(self, block):
        hmap = {"pe": "tensor", "act": "scalar", "dve": "vector", "pool": "gpsimd", "sp": "sync"}
        for e in self.engs:
            stream = self.streams[e]
            if not stream:
                continue

            def body(engine, stream=stream):
                for waits, fn, semname, inc in stream:
                    for (e2, v) in waits:
                        engine.wait_ge(self.sems[e2], v)
                    if fn is not None:
                        fn(engine).then_inc(self.sems[semname], inc)

            getattr(block, hmap[e])(body)


class Scratch:
    def __init__(self, ap_bf16, n):
        self.base = ap_bf16
        self.n = n
        self.off = 0

    def reset(self):
        self.off = 0

    def alloc(self, shape, dt):
        n = int(np.prod(shape))
        nb = n * 2 if dt == F32 else n
        nb = (nb + 1) // 2 * 2
        assert self.off + nb <= self.n, (self.off, nb, self.n)
        v = self.base[:, self.off:self.off + nb]
        self.off += nb
        if dt == F32:
            v = v.bitcast(F32)
        v = v[:, 0:n]
        if len(shape) == 2:
            v = v.rearrange("p (a b) -> p a b", a=shape[0])
        elif len(shape) == 3:
            v = v.rearrange("p (a b c) -> p a b c", a=shape[0], b=shape[1])
        return B(v)


def build(NT):
    nc = bass.Bass("TRN2", target_bir_lowering=False)
    SC = NT * T

    def din(name, shape, dt=F32):
        return nc.dram_tensor(name, shape, dt, kind="ExternalInput").ap()

    x_d = din("x", [SC, D])
    pos_d = din("pos", [1, SC], I32)
    pv_d = din("pv", [L, 128, NPV])
    cst_d = din("cst", [128, NCST])
    ada_w = din("ada_w", [L, D, 9 * D])
    w1 = [din("ffn1_w1", [L, D, DFF]), din("ffn2_w1", [L, D, DFF])]
    w3 = [din("ffn1_w3", [L, D, DFF]), din("ffn2_w3", [L, D, DFF])]
    w2 = [din("ffn1_w2", [L, DFF, D]), din("ffn2_w2", [L, DFF, D])]
    w_in = din("w_in", [L, D, 13344])
    w_out = din("w_out", [L, 4096, D])
    out_d = nc.dram_tensor("out", [SC, D], F32, kind="ExternalOutput").ap()
    DBG = os.environ.get("KDBG", "full")
    dbg_d = nc.dram_tensor("dbg", [128, 4096], F32, kind="ExternalOutput").ap() if "d" in DBG else None
    ss_d = nc.dram_tensor("ss_d", [L * 4, 128, 512], F32, kind="Internal").ap()
    rs_d = nc.dram_tensor("rs_d", [L * 8, 128, 512], F32, kind="Internal").ap()

    lgam = [math.log1p(-2.0 ** (-5.0 - h)) for h in range(8)]

    with contextlib.ExitStack() as st:
        sc = Sched(nc, st)
        sb = lambda name, shape, dt: st.enter_context(nc.sbuf_tensor("sb_" + name, shape, dt))
        hT_ = sb("hT", [128, 16, T], F32)
        nT_ = sb("nT", [128, 16, T], BF16)
        SCRN = 32 * 1024
        scr_ = sb("scr", [128, SCRN], BF16)
        scr = Scratch(scr_, SCRN)
        NSLOT = 3
        wslots = [sb(f"wslot{i}", [128, 8192], BF16) for i in range(NSLOT)]
        wtok = [Tok() for _ in range(NSLOT)]
        for i in range(NSLOT):
            sc.newsem(f"dw{i}", 16)
        sc.newsem("dl", 16)
        sc.newsem("ds", 16)
        cst = sb("cst", [128, NCST], F32)
        cst_t = Tok()
        cst16 = sb("cst16", [128, 384], BF16)
        pv = sb("pv", [128, L, NPV], F32)
        pv_t = Tok()
        modv = sb("modv", [128, L, 144], F32)
        coef = sb("coef", [128, L, 96], F32)
        aneg = sb("aneg", [128, L, 128], F32)
        coef_t = Tok()
        cond16 = sb("cond16", [128, 16], BF16)
        cosb = sb("cosb", [128, T], F32)
        sinb = sb("sinb", [128, T], F32)
        cs_t = Tok()
        halo = sb("halo", [128, L, 24, 3], F32)
        halo_t = [[Tok() for _ in range(24)] for _ in range(L)]
        psb = [st.enter_context(nc.psum_tensor(f"ps{i}", [128, 512], F32)) for i in range(8)]
        pst = [Tok() for _ in range(8)]
        hT_t = [Tok() for _ in range(16)]
        nT_t = [Tok() for _ in range(16)]
        state = {"slot": 0, "bank": 2}

        ident = cst[:, C_ID:C_ID + 128]
        tri = cst[:, C_TRI:C_TRI + 128]
        ones = cst[:, C_ONE:C_ONE + 128]
        negm = cst[:, C_NEG:C_NEG + 512]
        iota = cst[:, C_IOTA:C_IOTA + 512]
        ifq = cst[:, C_IFQ:C_IFQ + 1]
        ident16 = cst16[:, 0:128]
        ones16 = cst16[:, 128:256]

        def bank():
            b = state["bank"]
            state["bank"] = 2 + (b - 2 + 1) % 6
            return B(psb[b][:, :], pst[b])

        def wslab(src_ap, kc, cols):
            i = state["slot"]
            state["slot"] = (i + 1) % NSLOT
            v = wslots[i][:, 0:kc * cols].rearrange("p (k c) -> p k c", k=kc)
            q = "pool"
            sc.dma(q, f"dw{i}", lambda e: e.dma_start(out=v, in_=src_ap), writes=[wtok[i]])
            return B(v, wtok[i])

        def PE(fn, r, w):
            sc.op("pe", fn, reads=[b.t for b in r], writes=[b.t for b in w])

        def ACT(fn, r, w):
            sc.op("act", fn, reads=[b.t for b in r], writes=[b.t for b in w])

        def DVE(fn, r, w):
            sc.op("dve", fn, reads=[b.t for b in r], writes=[b.t for b in w])

        def mm(out, lhsT, rhs, start, stop, r, w):
            sc.pe_chain = not start
            PE(lambda e: e.matmul(out, lhsT, rhs, start=start, stop=stop), r, w)
            sc.pe_chain = False

        def dump(b, ap, c0):
            if dbg_d is None:
                return
            n = ap.shape[-1]
            sc.dma("sp", "ds", lambda e: e.dma_start(out=dbg_d[:, c0:c0 + n], in_=ap), reads=[b.t])

        CST = B(cst[:, :], cst_t)
        PV = B(pv[:, :, :], pv_t)
        COEF = B(coef[:, :, :], coef_t)
        CS = B(cosb[:, :], cs_t)
        hTb = [B(hT_[:, dc, :], hT_t[dc]) for dc in range(16)]
        nTb = [B(nT_[:, dc, :], nT_t[dc]) for dc in range(16)]

        sc.dma("sp", "dl", lambda e: e.dma_start(out=cst[:, :], in_=cst_d), writes=[cst_t])
        sc.dma("sp", "dl", lambda e: e.dma_start(out=pv[:, :, :], in_=pv_d.rearrange("l p n -> p l n")), writes=[pv_t])
        DVE(lambda e: e.tensor_copy(out=cst16[:, 0:128], in_=ident), [CST], [CST])
        DVE(lambda e: e.tensor_copy(out=cst16[:, 128:256], in_=ones), [CST], [CST])
        DVE(lambda e: e.memset(halo[:, :, :, :], 0.0), [], [B(None, t) for l_ in halo_t for t in l_])
        zb = scr.alloc([512], F32)
        DVE(lambda e: e.memset(zb.ap, 0.0), [], [zb])
        for i in range(L * 4):
            sc.dma("sp", "ds", lambda e, i=i: e.dma_start(out=ss_d[i], in_=zb.ap), reads=[zb.t])
        for i in range(L * 8):
            sc.dma("sp", "ds", lambda e, i=i: e.dma_start(out=rs_d[i], in_=zb.ap), reads=[zb.t])
        ACT(lambda e: e.activation(out=cond16[:, :], in_=pv[:, 0, P_C:P_C + 16], func=AF.Silu), [PV], [COEF])
        for l in range(L):
            pb = bank()
            for s in range(36):
                W = wslab(ada_w[l][:, 512 * s:512 * s + 512].rearrange("(k p) n -> p k n", p=128), 16, 512)
                for cc in range(4):
                    j = 4 * s + cc
                    for k in range(16):
                        mm(pb.ap[:, j:j + 1], W.ap[:, k, cc * 128:cc * 128 + 128], cond16[:, k:k + 1],
                           k == 0, k == 15, [W, COEF], [pb])
            DVE(lambda e, l=l, pb=pb: e.tensor_tensor(out=modv[:, l, :], in0=pb.ap[:, 0:144], in1=pv[:, l, P_ADAB:P_ADAB + 144], op=ALU.add),
                [pb, PV], [COEF])
            for i, (pn, scj) in enumerate([(P_N1, 1), (P_N2, 4), (P_N3, 7)]):
                DVE(lambda e, l=l, i=i, pn=pn, scj=scj: e.scalar_tensor_tensor(
                    out=coef[:, l, 16 * i:16 * i + 16], in0=modv[:, l, 16 * scj:16 * scj + 16], scalar=1.0,
                    in1=pv[:, l, pn:pn + 16], op0=ALU.add, op1=ALU.mult), [COEF, PV], [COEF])
            for i, (gj, f) in enumerate([(2, 0.5), (5, 1.0), (8, 0.5)]):
                DVE(lambda e, l=l, i=i, gj=gj, f=f: e.tensor_scalar(
                    out=coef[:, l, 48 + 16 * i:48 + 16 * i + 16], in0=modv[:, l, 16 * gj:16 * gj + 16],
                    scalar1=f, scalar2=None, op0=ALU.mult), [COEF], [COEF])
            ACT(lambda e, l=l: e.activation(out=aneg[:, l, :], in_=pv[:, l, P_ALOG:P_ALOG + 128], func=AF.Exp), [PV], [COEF])
            DVE(lambda e, l=l: e.tensor_scalar(out=aneg[:, l, :], in0=aneg[:, l, :], scalar1=-1.0, scalar2=None, op0=ALU.mult),
                [COEF], [COEF])
        sc.newsem("dc", 16)

        def cache(name, src):
            Lx, R, C = src.shape
            dst = nc.dram_tensor(name + "_b16", [Lx, R, C], BF16, kind="Internal").ap()
            for l_ in range(Lx):
                for r0 in range(0, R, 128):
                    sc.dma("pool", "dc", lambda e: e.dma_start(out=dst[l_][r0:r0 + 128, :], in_=src[l_][r0:r0 + 128, :]))
            return dst

        if os.environ.get("KNOCACHE") is None:
            for wi in range(2):
                w1[wi] = cache(f"w1_{wi}", w1[wi])
                w3[wi] = cache(f"w3_{wi}", w3[wi])
                w2[wi] = cache(f"w2_{wi}", w2[wi])
            w_in = cache("w_in", w_in)
            w_out = cache("w_out", w_out)
        sc.barrier()

        def rms_rstd(srcs):
            pb = bank()
            sq = [scr.alloc([T], BF16) for _ in range(2)]
            for dc in range(16):
                s = sq[dc % 2]
                ACT(lambda e, s=s, dc=dc: e.activation(out=s.ap, in_=srcs[dc].ap, func=AF.Square), [srcs[dc]], [s])
                mm(pb.ap, ones16, s.ap, dc == 0, dc == 15, [s, CST], [pb])
            rstd = scr.alloc([T], F32)
            ACT(lambda e: e.activation(out=rstd.ap, in_=pb.ap, func=AF.Sqrt, bias=EPS, scale=1.0 / D), [pb], [rstd])
            DVE(lambda e: e.reciprocal(out=rstd.ap, in_=rstd.ap), [rstd], [rstd])
            return rstd

        def rms_mod(l, i):
            rstd = rms_rstd(hTb)
            tmp = [scr.alloc([T], F32) for _ in range(2)]
            shj = [0, 3, 6][i]
            for dc in range(16):
                t_ = tmp[dc % 2]
                DVE(lambda e, dc=dc, t_=t_: e.scalar_tensor_tensor(out=t_.ap, in0=hTb[dc].ap, scalar=coef[:, l, 16 * i + dc:16 * i + dc + 1],
                                                                   in1=rstd.ap, op0=ALU.mult, op1=ALU.mult), [hTb[dc], rstd, COEF], [t_])
                ACT(lambda e, dc=dc, t_=t_: e.activation(out=nTb[dc].ap, in_=t_.ap, func=AF.Identity,
                                                         bias=modv[:, l, 16 * shj + dc:16 * shj + dc + 1]), [t_, COEF], [nTb[dc]])

        def resid_add(l, gi, dc, pb):
            DVE(lambda e: e.scalar_tensor_tensor(out=hTb[dc].ap, in0=pb.ap, scalar=coef[:, l, 48 + 16 * gi + dc:48 + 16 * gi + dc + 1],
                                                 in1=hTb[dc].ap, op0=ALU.mult, op1=ALU.add), [pb, hTb[dc], COEF], [hTb[dc]])

        def ffn(l, which):
            sc.barrier()
            scr.reset()
            rms_mod(l, 0 if which == 0 else 2)
            act = [scr.alloc([T], BF16) for _ in range(44)]
            sa = [scr.alloc([T], F32) for _ in range(2)]
            for s in range(11):
                W1 = wslab(w1[which][l][:, 512 * s:512 * s + 512].rearrange("(k p) n -> p k n", p=128), 16, 512)
                W3 = wslab(w3[which][l][:, 512 * s:512 * s + 512].rearrange("(k p) n -> p k n", p=128), 16, 512)
                for cc in range(4):
                    fc = 4 * s + cc
                    pa, pb = bank(), bank()
                    for k in range(16):
                        mm(pa.ap, W1.ap[:, k, cc * 128:cc * 128 + 128], nTb[k].ap, k == 0, k == 15, [W1, nTb[k]], [pa])
                    for k in range(16):
                        mm(pb.ap, W3.ap[:, k, cc * 128:cc * 128 + 128], nTb[k].ap, k == 0, k == 15, [W3, nTb[k]], [pb])
                    s_ = sa[fc % 2]
                    ACT(lambda e, s_=s_, pa=pa: e.activation(out=s_.ap, in_=pa.ap, func=AF.Silu), [pa], [s_])
                    DVE(lambda e, s_=s_, pb=pb, fc=fc: e.tensor_tensor(out=act[fc].ap, in0=pb.ap, in1=s_.ap, op=ALU.mult), [pb, s_], [act[fc]])
            for dc in range(16):
                W2 = wslab(w2[which][l][:, 128 * dc:128 * dc + 128].rearrange("(k p) n -> p k n", p=128), 44, 128)
                pb = bank()
                for k in range(44):
                    mm(pb.ap, W2.ap[:, k, :], act[k].ap, k == 0, k == 43, [W2, act[k]], [pb])
                resid_add(l, 0 if which == 0 else 2, dc, pb)

        def proj_fm(l, col0, ncols_chunks, W, wc0, evac):
            for i in range(ncols_chunks):
                pb = bank()
                for k in range(16):
                    mm(pb.ap, W.ap[:, k, wc0 + i * 128:wc0 + i * 128 + 128], nTb[k].ap, k == 0, k == 15, [W, nTb[k]], [pb])
                evac(pb, i)

        def win_slab(l, c0, ncols):
            return wslab(w_in[l][:, c0:c0 + ncols].rearrange("(k p) n -> p k n", p=128), 16, ncols)

        def ssd(l):
            sc.barrier()
            scr.reset()
            rms_mod(l, 1)
            A = scr.alloc
            Wdt = win_slab(l, DTO, 32)
            pb = bank()
            for blk in range(4):
                for k in range(16):
                    mm(pb.ap[:, blk * 32:blk * 32 + 32], nT_[:, k, blk * 128:blk * 128 + 128], Wdt.ap[:, k, :], k == 0, k == 15, [Wdt, nTb[k]], [pb])
            xd = A([128], F32)
            ax = A([128], F32)
            dt = A([128], F32)
            dA = A([128], F32)
            acum = A([128], F32)
            nacum = A([128], F32)
            atot = A([128], F32)
            decin = A([128], F32)
            dtdo = A([128], F32)
            cdec = A([128], F32)
            if dbg_d is not None:
                wd32 = A([512], F32)
                DVE(lambda e: e.tensor_copy(out=wd32.ap, in_=Wdt.ap.rearrange("p k c -> p (k c)")), [Wdt], [wd32])
                dump(wd32, wd32.ap, 2560)
                n32 = A([512], F32)
                DVE(lambda e: e.tensor_copy(out=n32.ap, in_=nT_[:, 0, :]), [nTb[0]], [n32])
                dump(n32, n32.ap, 3072)
                dr = A([128], F32)
                DVE(lambda e: e.tensor_copy(out=dr.ap, in_=pb.ap[:, 0:128]), [pb], [dr])
                dump(dr, dr.ap, 3584)
            DVE(lambda e: e.tensor_tensor(out=xd.ap, in0=pb.ap[:, 0:128], in1=pv[:, l, P_DTB:P_DTB + 128], op=ALU.add), [pb, PV], [xd])
            DVE(lambda e: e.tensor_scalar(out=ax.ap, in0=xd.ap, scalar1=-1.0, scalar2=None, op0=ALU.mult), [xd], [ax])
            DVE(lambda e: e.tensor_tensor(out=ax.ap, in0=ax.ap, in1=xd.ap, op=ALU.max), [ax, xd], [ax])
            ACT(lambda e: e.activation(out=ax.ap, in_=ax.ap, func=AF.Exp, scale=-1.0), [ax], [ax])
            ACT(lambda e: e.activation(out=ax.ap, in_=ax.ap, func=AF.Ln, bias=1.0), [ax], [ax])
            DVE(lambda e: e.scalar_tensor_tensor(out=dt.ap, in0=xd.ap, scalar=0.0, in1=ax.ap, op0=ALU.max, op1=ALU.add), [xd, ax], [dt])
            DVE(lambda e: e.tensor_tensor(out=dA.ap, in0=dt.ap, in1=aneg[:, l, :], op=ALU.mult), [dt, COEF], [dA])
            pc = bank()
            for blk in range(4):
                mm(pc.ap[:, blk * 32:blk * 32 + 32], tri, dA.ap[:, blk * 32:blk * 32 + 32], True, True, [dA, CST], [pc])
            mm(pc.ap[:, 128:256], ones, dA.ap, True, True, [dA, CST], [pc])
            DVE(lambda e: e.tensor_copy(out=acum.ap, in_=pc.ap[:, 0:128]), [pc], [acum])
            DVE(lambda e: e.tensor_scalar(out=nacum.ap, in0=pc.ap[:, 0:128], scalar1=-1.0, scalar2=None, op0=ALU.mult), [pc], [nacum])
            DVE(lambda e: e.tensor_copy(out=atot.ap, in_=pc.ap[:, 128:256]), [pc], [atot])
            ACT(lambda e: e.activation(out=decin.ap, in_=acum.ap, func=AF.Exp), [acum], [decin])
            DVE(lambda e: e.tensor_tensor(out=dtdo.ap, in0=atot.ap, in1=acum.ap, op=ALU.subtract), [atot, acum], [dtdo])
            ACT(lambda e: e.activation(out=dtdo.ap, in_=dtdo.ap, func=AF.Exp), [dtdo], [dtdo])
            DVE(lambda e: e.tensor_tensor(out=dtdo.ap, in0=dtdo.ap, in1=dt.ap, op=ALU.mult), [dtdo, dt], [dtdo])
            ACT(lambda e: e.activation(out=cdec.ap, in_=atot.ap, func=AF.Exp), [atot], [cdec])

            for i_, b_ in enumerate([xd, dt, dA, acum, atot, decin, dtdo, cdec]):
                dump(b_, b_.ap, 128 * i_)
            ubuf = A([T + 4], F32)
            acc = A([T], F32)
            xT = A([4, T], BF16)
            BT = A([T], BF16)
            CT = A([T], BF16)
            xtok = A([4, 512], BF16)
            btok = A([4, 128], BF16)
            zs = A([4, 512], BF16)
            S = A([512], F32)
            S16 = A([512], BF16)
            cbT = A([128], F32)
            R = A([8, 128], F32)
            Lm = A([8, 128], F32)
            M = A([8, 128], BF16)
            xdt = A([512], BF16)
            xdd = A([512], BF16)
            y1 = A([512], F32)
            y2 = A([512], F32)
            ssq = A([2], F32)
            yn = A([512], BF16)
            yT = A([4, T], BF16)
            junk = A([512], BF16)

            def b3(ap2, h0):
                return ap2.unsqueeze(2).broadcast_to([128, 8, 64])

            def v3(ap2):
                return ap2.rearrange("p (h q) -> p h q", h=8)

            for g in range(4):
                Wx = win_slab(l, XO + 512 * g, 512)
                WB = win_slab(l, BO + 128 * g, 128)
                WC = win_slab(l, CO + 128 * g, 128)
                jobs = [(Wx, i * 128, 4 * g + i, xT.ap[:, i, :]) for i in range(4)] + [(WB, 0, 16 + g, BT.ap), (WC, 0, 20 + g, CT.ap)]
                for (W, wc0, ci, dst) in jobs:
                    dstB = xT if W is Wx else (BT if W is WB else CT)
                    HB = B(None, halo_t[l][ci])
                    pb = bank()
                    for k in range(16):
                        mm(pb.ap, W.ap[:, k, wc0:wc0 + 128], nTb[k].ap, k == 0, k == 15, [W, nTb[k]], [pb])
                    ACT(lambda e, pb=pb: e.activation(out=ubuf.ap[:, 3:3 + T], in_=pb.ap, func=AF.Identity), [pb], [ubuf])
                    DVE(lambda e, ci=ci: e.tensor_copy(out=ubuf.ap[:, 0:3], in_=halo[:, l, ci, :]), [HB], [ubuf])
                    DVE(lambda e, ci=ci: e.tensor_copy(out=halo[:, l, ci, :], in_=ubuf.ap[:, T:T + 3]), [ubuf], [HB])
                    cw = lambda k_, ci=ci: pv[:, l, P_CW + k_ * 24 + ci:P_CW + k_ * 24 + ci + 1]
                    DVE(lambda e, ci=ci, cw=cw: e.tensor_scalar(out=acc.ap, in0=ubuf.ap[:, 0:T], scalar1=cw(0), scalar2=pv[:, l, P_CB + ci:P_CB + ci + 1],
                                                                op0=ALU.mult, op1=ALU.add), [ubuf, PV], [acc])
                    for k_ in range(1, 4):
                        DVE(lambda e, k_=k_, cw=cw: e.scalar_tensor_tensor(out=acc.ap, in0=ubuf.ap[:, k_:k_ + T], scalar=cw(k_), in1=acc.ap,
                                                                           op0=ALU.mult, op1=ALU.add), [ubuf, acc, PV], [acc])
                    ACT(lambda e, dst=dst: e.activation(out=dst, in_=acc.ap, func=AF.Silu), [acc], [dstB])
                Wz = win_slab(l, ZO + 512 * g, 512)
                for blk in range(4):
                    pb = bank()
                    for k in range(16):
                        mm(pb.ap, nT_[:, k, blk * 128:blk * 128 + 128], Wz.ap[:, k, :], k == 0, k == 15, [Wz, nTb[k]], [pb])
                    ACT(lambda e, pb=pb, blk=blk: e.activation(out=zs.ap[:, blk, :], in_=pb.ap, func=AF.Silu), [pb], [zs])
                for blk in range(4):
                    pb = bank()
                    pbv = pb.ap.bitcast(BF16)
                    for i in range(4):
                        PE(lambda e, pbv=pbv, i=i, blk=blk: e.transpose(pbv[:, i * 128:i * 128 + 128], xT.ap[:, i, blk * 128:blk * 128 + 128], ident16),
                           [xT, CST], [pb])
                    PE(lambda e, pbv=pbv, blk=blk: e.transpose(pbv[:, 512:640], BT.ap[:, blk * 128:blk * 128 + 128], ident16), [BT, CST], [pb])
                    DVE(lambda e, pbv=pbv, blk=blk: e.tensor_copy(out=xtok.ap[:, blk, :], in_=pbv[:, 0:512]), [pb], [xtok])
                    DVE(lambda e, pbv=pbv, blk=blk: e.tensor_copy(out=btok.ap[:, blk, :], in_=pbv[:, 512:640]), [pb], [btok])
                sc.dma("sp", "dl", lambda e, g=g: e.dma_start(out=S.ap, in_=ss_d[l * 4 + g]), writes=[S.t])
                DVE(lambda e: e.tensor_copy(out=S16.ap, in_=S.ap), [S], [S16])
                for blk in range(4):
                    bs = slice(blk * 128, blk * 128 + 128)
                    hs = slice(blk * 32 + 8 * g, blk * 32 + 8 * g + 8)
                    pcb = bank()
                    mm(pcb.ap[:, 0:128], BT.ap[:, bs], CT.ap[:, bs], True, True, [BT, CT], [pcb])
                    DVE(lambda e, pcb=pcb: e.tensor_copy(out=cbT.ap, in_=pcb.ap[:, 0:128]), [pcb], [cbT])
                    DVE(lambda e, hs=hs: e.tensor_tensor(out=R.ap, in0=dA.ap[:, hs].unsqueeze(2).broadcast_to([128, 8, 128]),
                                                         in1=tri.unsqueeze(1).broadcast_to([128, 8, 128]), op=ALU.mult), [dA, CST], [R])
                    for hb in range(2):
                        pseg = bank()
                        mm(pseg.ap, ones, R.ap[:, 4 * hb:4 * hb + 4, :], True, False, [R, CST], [pseg])
                        mm(pseg.ap, ident, negm, False, True, [CST], [pseg])
                        for hh in range(4):
                            h = 4 * hb + hh
                            ACT(lambda e, pseg=pseg, hh=hh, h=h, blk=blk: e.activation(
                                out=Lm.ap[:, h, :], in_=pseg.ap[:, hh * 128:hh * 128 + 128], func=AF.Exp,
                                bias=nacum.ap[:, blk * 32 + 8 * g + h:blk * 32 + 8 * g + h + 1]), [pseg, nacum], [Lm])
                    DVE(lambda e: e.tensor_tensor(out=M.ap, in0=Lm.ap, in1=cbT.ap.unsqueeze(1).broadcast_to([128, 8, 128]), op=ALU.mult), [Lm, cbT], [M])
                    DVE(lambda e, blk=blk, hs=hs: e.tensor_tensor(out=v3(xdt.ap), in0=v3(xtok.ap[:, blk, :]), in1=b3(dt.ap[:, hs], 0), op=ALU.mult),
                        [xtok, dt], [xdt])
                    DVE(lambda e, blk=blk, hs=hs: e.tensor_tensor(out=v3(xdd.ap), in0=v3(xtok.ap[:, blk, :]), in1=b3(dtdo.ap[:, hs], 0), op=ALU.mult),
                        [xtok, dtdo], [xdd])
                    py = bank()
                    for h in range(8):
                        mm(py.ap[:, h * 64:h * 64 + 64], M.ap[:, h, :], xdt.ap[:, h * 64:h * 64 + 64], True, True, [M, xdt], [py])
                    po = bank()
                    mm(po.ap, CT.ap[:, bs], S16.ap, True, True, [CT, S16], [po])
                    DVE(lambda e, po=po, hs=hs: e.tensor_tensor(out=v3(y1.ap), in0=v3(po.ap), in1=b3(decin.ap[:, hs], 0), op=ALU.mult), [po, decin], [y1])
                    DVE(lambda e, py=py: e.tensor_tensor(out=y2.ap, in0=py.ap, in1=y1.ap, op=ALU.add), [py, y1], [y2])
                    DVE(lambda e, blk=blk: e.tensor_tensor(out=v3(y1.ap), in0=v3(xtok.ap[:, blk, :]),
                                                           in1=b3(pv[:, l, P_DSK + 8 * g:P_DSK + 8 * g + 8], 0), op=ALU.mult), [xtok, PV], [y1])
                    DVE(lambda e: e.tensor_tensor(out=y2.ap, in0=y2.ap, in1=y1.ap, op=ALU.add), [y2, y1], [y2])
                    DVE(lambda e, blk=blk: e.tensor_tensor(out=y2.ap, in0=y2.ap, in1=zs.ap[:, blk, :], op=ALU.mult), [y2, zs], [y2])
                    if g == 0 and blk == 0:
                        dump(cbT, cbT.ap, 1024)
                        dump(Lm, Lm.ap[:, 0, :], 1152)
                        dump(y2, y2.ap, 1280)
                    DVE(lambda e: e.memset(ssq.ap, 0.0), [], [ssq])
                    ACT(lambda e: e.activation(out=junk.ap, in_=y2.ap, func=AF.Square, accum_out=ssq.ap[:, 0:1]), [y2], [junk, ssq])
                    ACT(lambda e: e.activation(out=ssq.ap[:, 1:2], in_=ssq.ap[:, 0:1], func=AF.Sqrt, bias=EPS, scale=1.0 / 512), [ssq], [ssq])
                    DVE(lambda e: e.reciprocal(out=ssq.ap[:, 1:2], in_=ssq.ap[:, 1:2]), [ssq], [ssq])
                    DVE(lambda e: e.tensor_scalar(out=yn.ap, in0=y2.ap, scalar1=ssq.ap[:, 1:2], scalar2=None, op0=ALU.mult), [y2, ssq], [yn])
                    if g == 0 and blk == 0:
                        dump(ssq, ssq.ap, 1792)
                        dump(S, S.ap, 2048)
                    pt = bank()
                    ptv = pt.ap.bitcast(BF16)
                    for i in range(4):
                        PE(lambda e, ptv=ptv, i=i: e.transpose(ptv[:, i * 128:i * 128 + 128], yn.ap[:, i * 128:i * 128 + 128], ident16), [yn, CST], [pt])
                    for i in range(4):
                        ACT(lambda e, ptv=ptv, i=i, bs=bs: e.activation(out=yT.ap[:, i, bs], in_=ptv[:, i * 128:i * 128 + 128], func=AF.Identity,
                                                                       scale=pv[:, l, P_SNW + 4 * g + i:P_SNW + 4 * g + i + 1]), [pt, PV], [yT])
                    pS = bank()
                    mm(pS.ap, btok.ap[:, blk, :], xdd.ap, True, True, [btok, xdd], [pS])
                    DVE(lambda e, hs=hs: e.tensor_tensor(out=v3(S.ap), in0=v3(S.ap), in1=b3(cdec.ap[:, hs], 0), op=ALU.mult), [S, cdec], [S])
                    DVE(lambda e, pS=pS: e.tensor_tensor(out=S.ap, in0=S.ap, in1=pS.ap, op=ALU.add), [S, pS], [S])
                    DVE(lambda e: e.tensor_copy(out=S16.ap, in_=S.ap), [S], [S16])
                sc.dma("sp", "ds", lambda e, g=g: e.dma_start(out=ss_d[l * 4 + g], in_=S.ap), reads=[S.t])
                Wo = wslab(w_out[l][512 * g:512 * g + 512, :].rearrange("(k p) n -> p k n", p=128), 4, 2048)
                for dc in range(16):
                    pb = bank()
                    for i in range(4):
                        mm(pb.ap, Wo.ap[:, i, dc * 128:dc * 128 + 128], yT.ap[:, i, :], i == 0, i == 3, [Wo, yT], [pb])
                    resid_add(l, 1, dc, pb)

        def ret(l):
            sc.barrier()
            scr.reset()
            A = scr.alloc
            qraw = A([2, T], F32)
            t1 = A([T], F32)
            t2 = A([T], F32)
            cq = A([T], F32)
            sq_ = A([T], F32)
            qs = A([2, T], BF16)
            ks = A([2, T], BF16)
            vtok = A([4, 256], BF16)
            kstok = A([4, 256], BF16)
            gs = A([2, T], BF16)
            S = A([2, 256], F32)
            Sg = A([2, 256], BF16)
            stmp = A([256], F32)
            PT = [A([T], BF16) for _ in range(2)]
            sqb = A([T], BF16)
            rstd = A([T], F32)
            yfin = A([2, T], BF16)
            yps = [B(psb[0][:, :], pst[0]), B(psb[1][:, :], pst[1])]
            for hh in range(8):
                lg = lgam[hh]
                dfix = cst[:, C_DFIX + 128 * hh:C_DFIX + 128 * hh + 128]
                for which, off, dst in ((0, QO, qs), (1, KO, ks)):
                    W = win_slab(l, off + 256 * hh, 256)
                    proj_fm(l, 0, 2, W, 0, lambda pb, i: ACT(lambda e, pb=pb, i=i: e.activation(out=qraw.ap[:, i, :], in_=pb.ap, func=AF.Identity), [pb], [qraw]))
                    if which == 0:
                        ACT(lambda e, lg=lg: e.activation(out=t1.ap, in_=iota, func=AF.Exp, scale=lg), [CST], [t1])
                    else:
                        ACT(lambda e, lg=lg: e.activation(out=t1.ap, in_=iota, func=AF.Exp, scale=-lg, bias=math.log(1.0 / 16.0)), [CST], [t1])
                    DVE(lambda e: e.tensor_tensor(out=cq.ap, in0=cosb[:, :], in1=t1.ap, op=ALU.mult), [CS, t1], [cq])
                    DVE(lambda e: e.tensor_tensor(out=sq_.ap, in0=sinb[:, :], in1=t1.ap, op=ALU.mult), [CS, t1], [sq_])
                    DVE(lambda e: e.tensor_tensor(out=t1.ap, in0=qraw.ap[:, 0, :], in1=cq.ap, op=ALU.mult), [qraw, cq], [t1])
                    DVE(lambda e: e.tensor_tensor(out=t2.ap, in0=qraw.ap[:, 1, :], in1=sq_.ap, op=ALU.mult), [qraw, sq_], [t2])
                    DVE(lambda e, dst=dst: e.tensor_tensor(out=dst.ap[:, 0, :], in0=t1.ap, in1=t2.ap, op=ALU.subtract), [t1, t2], [dst])
                    DVE(lambda e: e.tensor_tensor(out=t1.ap, in0=qraw.ap[:, 0, :], in1=sq_.ap, op=ALU.mult), [qraw, sq_], [t1])
                    DVE(lambda e: e.tensor_tensor(out=t2.ap, in0=qraw.ap[:, 1, :], in1=cq.ap, op=ALU.mult), [qraw, cq], [t2])
                    DVE(lambda e, dst=dst: e.tensor_tensor(out=dst.ap[:, 1, :], in0=t1.ap, in1=t2.ap, op=ALU.add), [t1, t2], [dst])
                Wv = win_slab(l, VO + 256 * hh, 256)
                for bp in range(2):
                    pb = bank()
                    for b2 in range(2):
                        blk = 2 * bp + b2
                        for k in range(16):
                            mm(pb.ap[:, b2 * 256:b2 * 256 + 256], nT_[:, k, blk * 128:blk * 128 + 128], Wv.ap[:, k, :], k == 0, k == 15, [Wv, nTb[k]], [pb])
                    ACT(lambda e, pb=pb, bp=bp: e.activation(out=vtok.ap[:, 2 * bp:2 * bp + 2, :], in_=pb.ap.rearrange("p (a b) -> p a b", a=2), func=AF.Identity),
                        [pb], [vtok])
                Wg = win_slab(l, GO + 256 * hh, 256)
                proj_fm(l, 0, 2, Wg, 0, lambda pb, i: ACT(lambda e, pb=pb, i=i: e.activation(out=gs.ap[:, i, :], in_=pb.ap, func=AF.Silu), [pb], [gs]))
                for bp in range(2):
                    pb = bank()
                    pbv = pb.ap.bitcast(BF16)
                    for b2 in range(2):
                        blk = 2 * bp + b2
                        for dh in range(2):
                            PE(lambda e, pbv=pbv, b2=b2, dh=dh, blk=blk: e.transpose(pbv[:, b2 * 256 + dh * 128:b2 * 256 + dh * 128 + 128],
                                                                                       ks.ap[:, dh, blk * 128:blk * 128 + 128], ident16), [ks, CST], [pb])
                    DVE(lambda e, pbv=pbv, bp=bp: e.tensor_copy(out=kstok.ap[:, 2 * bp:2 * bp + 2, :], in_=pbv[:, 0:512].rearrange("p (a b) -> p a b", a=2)),
                        [pb], [kstok])
                sc.dma("sp", "dl", lambda e, hh=hh: e.dma_start(out=S.ap, in_=rs_d[l * 8 + hh].rearrange("p (a b) -> p a b", a=2)), writes=[S.t])
                DVE(lambda e, lg=lg: e.tensor_scalar(out=Sg.ap, in0=S.ap, scalar1=math.exp(lg), scalar2=None, op0=ALU.mult), [S], [Sg])
                for eh in range(2):
                    for dh in range(2):
                        mm(yps[eh].ap, Sg.ap[:, dh, eh * 128:eh * 128 + 128], qs.ap[:, dh, :], dh == 0, False, [Sg, qs], [yps[eh]])
                for sb_ in range(4):
                    nl = T - 128 * sb_
                    sps = bank()
                    for dh in range(2):
                        mm(sps.ap[:, 0:nl], ks.ap[:, dh, sb_ * 128:sb_ * 128 + 128], qs.ap[:, dh, 128 * sb_:T], dh == 0, dh == 1, [ks, qs], [sps])
                    P_ = PT[sb_ % 2]
                    DVE(lambda e, sps=sps, P_=P_, dfix=dfix: e.tensor_tensor(out=P_.ap[:, 0:128], in0=sps.ap[:, 0:128], in1=dfix, op=ALU.mult), [sps, CST], [P_])
                    if nl > 128:
                        ACT(lambda e, sps=sps, P_=P_, nl=nl: e.activation(out=P_.ap[:, 128:nl], in_=sps.ap[:, 128:nl], func=AF.Identity), [sps], [P_])
                    for eh in range(2):
                        mm(yps[eh].ap[:, 128 * sb_:T], vtok.ap[:, sb_, eh * 128:eh * 128 + 128], P_.ap[:, 0:nl], False, sb_ == 3, [vtok, P_], [yps[eh]])
                for dh in range(2):
                    pS = bank()
                    for blk in range(4):
                        mm(pS.ap[:, 0:256], kstok.ap[:, blk, dh * 128:dh * 128 + 128], vtok.ap[:, blk, :], blk == 0, blk == 3, [kstok, vtok], [pS])
                    ACT(lambda e, pS=pS, lg=lg: e.activation(out=stmp.ap, in_=pS.ap[:, 0:256], func=AF.Identity, scale=math.exp(lg * (T - 1))), [pS], [stmp])
                    DVE(lambda e, dh=dh, lg=lg: e.scalar_tensor_tensor(out=S.ap[:, dh, :], in0=S.ap[:, dh, :], scalar=math.exp(lg * T), in1=stmp.ap,
                                                                       op0=ALU.mult, op1=ALU.add), [S, stmp], [S])
                sc.dma("sp", "ds", lambda e, hh=hh: e.dma_start(out=rs_d[l * 8 + hh].rearrange("p (a b) -> p a b", a=2), in_=S.ap), reads=[S.t])
                pm = bank()
                for eh in range(2):
                    ACT(lambda e, eh=eh: e.activation(out=sqb.ap, in_=yps[eh].ap, func=AF.Square), [yps[eh]], [sqb])
                    mm(pm.ap, ones16, sqb.ap, eh == 0, eh == 1, [sqb, CST], [pm])
                ACT(lambda e, pm=pm: e.activation(out=rstd.ap, in_=pm.ap, func=AF.Sqrt, bias=EPS, scale=1.0 / 256), [pm], [rstd])
                DVE(lambda e: e.reciprocal(out=rstd.ap, in_=rstd.ap), [rstd], [rstd])
                for eh in range(2):
                    DVE(lambda e, eh=eh: e.tensor_tensor(out=t1.ap, in0=yps[eh].ap, in1=rstd.ap, op=ALU.mult), [yps[eh], rstd], [t1])
                    DVE(lambda e, eh=eh, hh=hh: e.scalar_tensor_tensor(out=yfin.ap[:, eh, :], in0=t1.ap, scalar=pv[:, l, P_RNW + 2 * hh + eh:P_RNW + 2 * hh + eh + 1],
                                                                       in1=gs.ap[:, eh, :], op0=ALU.mult, op1=ALU.mult), [t1, gs, PV], [yfin])
                Wo = wslab(w_out[l][2048 + 256 * hh:2048 + 256 * hh + 256, :].rearrange("(k p) n -> p k n", p=128), 2, 2048)
                for dc in range(16):
                    pb = bank()
                    for i in range(2):
                        mm(pb.ap, Wo.ap[:, i, dc * 128:dc * 128 + 128], yfin.ap[:, i, :], i == 0, i == 1, [Wo, yfin], [pb])
                    resid_add(l, 1, dc, pb)

        for ti in range(NT):
            sc.barrier()
            scr.reset()
            t0 = ti * T
            xt = [scr.alloc([D], F32) for _ in range(2)]
            for blk in range(4):
                xb = xt[blk % 2]
                sc.dma("sp", "dl", lambda e, xb=xb, blk=blk: e.dma_start(out=xb.ap, in_=x_d[t0 + blk * 128:t0 + blk * 128 + 128, :]), writes=[xb.t])
                for d4 in range(4):
                    pb = bank()
                    for i in range(4):
                        dc = 4 * d4 + i
                        mm(pb.ap[:, i * 128:i * 128 + 128], xb.ap[:, dc * 128:dc * 128 + 128], ident, True, True, [xb, CST], [pb])
                    ACT(lambda e, pb=pb, d4=d4, blk=blk: e.activation(out=hT_[:, 4 * d4:4 * d4 + 4, blk * 128:blk * 128 + 128],
                                                                      in_=pb.ap.rearrange("p (a b) -> p a b", a=4), func=AF.Identity),
                        [pb], [hTb[4 * d4 + i] for i in range(4)])
            pi_ = scr.alloc([T], I32) if False else None
            posi = scr.alloc([T], F32)
            posv = posi.ap.bitcast(I32)
            ang = scr.alloc([T], F32)
            r_ = scr.alloc([T], F32)
            sc.dma("sp", "dl", lambda e: e.dma_start(out=posv, in_=pos_d[0:1, t0:t0 + T].partition_broadcast(128)), writes=[posi.t])
            DVE(lambda e: e.tensor_copy(out=ang.ap, in_=posv), [posi], [ang])
            DVE(lambda e: e.tensor_scalar(out=ang.ap, in0=ang.ap, scalar1=ifq, scalar2=None, op0=ALU.mult), [ang, CST], [ang])
            kf = scr.alloc([T], F32)
            ki = kf.ap.bitcast(I32)
            m_ = scr.alloc([T], F32)
            C1 = 6.28125
            C2 = 2 * math.pi - C1
            DVE(lambda e: e.tensor_scalar(out=m_.ap, in0=ang.ap, scalar1=1.0 / (2 * math.pi), scalar2=None, op0=ALU.mult), [ang], [m_])
            DVE(lambda e: e.tensor_copy(out=ki, in_=m_.ap), [m_], [kf])
            DVE(lambda e: e.tensor_copy(out=m_.ap, in_=ki), [kf], [m_])
            DVE(lambda e: e.scalar_tensor_tensor(out=r_.ap, in0=m_.ap, scalar=-C1, in1=ang.ap, op0=ALU.mult, op1=ALU.add), [m_, ang], [r_])
            DVE(lambda e: e.scalar_tensor_tensor(out=r_.ap, in0=m_.ap, scalar=-C2, in1=r_.ap, op0=ALU.mult, op1=ALU.add), [m_, r_], [r_])

            def wrap(rb):
                DVE(lambda e: e.tensor_scalar(out=m_.ap, in0=rb.ap, scalar1=math.pi, scalar2=-2 * math.pi, op0=ALU.is_gt, op1=ALU.mult), [rb], [m_])
                DVE(lambda e: e.tensor_tensor(out=rb.ap, in0=rb.ap, in1=m_.ap, op=ALU.add), [rb, m_], [rb])
                DVE(lambda e: e.tensor_scalar(out=m_.ap, in0=rb.ap, scalar1=-math.pi, scalar2=2 * math.pi, op0=ALU.is_lt, op1=ALU.mult), [rb], [m_])
                DVE(lambda e: e.tensor_tensor(out=rb.ap, in0=rb.ap, in1=m_.ap, op=ALU.add), [rb, m_], [rb])
                DVE(lambda e: e.tensor_scalar(out=rb.ap, in0=rb.ap, scalar1=math.pi, scalar2=-math.pi, op0=ALU.min, op1=ALU.max), [rb], [rb])

            wrap(r_)
            ACT(lambda e: e.activation(out=sinb[:, :], in_=r_.ap, func=AF.Sin), [r_], [CS])
            DVE(lambda e: e.tensor_scalar(out=r_.ap, in0=r_.ap, scalar1=math.pi / 2, scalar2=None, op0=ALU.add), [r_], [r_])
            wrap(r_)
            ACT(lambda e: e.activation(out=cosb[:, :], in_=r_.ap, func=AF.Sin), [r_], [CS])
            dbg = os.environ.get("KDBG", "full")
            for l in range(L):
                if dbg == "full" or (l == 0 and "a" in dbg):
                    ffn(l, 0)
                if dbg == "full" or (l == 0 and "s" in dbg):
                    ssd(l)
                if dbg == "full" or (l == 0 and "r" in dbg):
                    if not (dbg == "full" or "s" in dbg):
                        sc.barrier()
                        scr.reset()
                        rms_mod(l, 1)
                    ret(l)
                if dbg == "full" or (l == 0 and "b" in dbg):
                    ffn(l, 1)
            sc.barrier()
            scr.reset()
            rstd = rms_rstd(hTb)
            o = [scr.alloc([T], F32) for _ in range(16)]
            for dc in range(16):
                DVE(lambda e, dc=dc: e.scalar_tensor_tensor(out=o[dc].ap, in0=hTb[dc].ap, scalar=pv[:, 0, P_FN + dc:P_FN + dc + 1], in1=rstd.ap,
                                                            op0=ALU.mult, op1=ALU.mult), [hTb[dc], rstd, PV], [o[dc]])
            ot = [scr.alloc([D], F32) for _ in range(2)]
            for blk in range(4):
                ob = ot[blk % 2]
                for d4 in range(4):
                    pb = bank()
                    for i in range(4):
                        dc = 4 * d4 + i
                        mm(pb.ap[:, i * 128:i * 128 + 128], o[dc].ap[:, blk * 128:blk * 128 + 128], ident, True, True, [o[dc], CST], [pb])
                    ACT(lambda e, pb=pb, ob=ob, d4=d4: e.activation(out=ob.ap[:, d4 * 512:d4 * 512 + 512], in_=pb.ap, func=AF.Identity), [pb], [ob])
                sc.dma("sp", "ds", lambda e, ob=ob, blk=blk: e.dma_start(out=out_d[t0 + blk * 128:t0 + blk * 128 + 128, :], in_=ob.ap), reads=[ob.t])
        sc.barrier()
        with nc.Block() as block:
            sc.emit(block)
    return nc


def make_consts():
    c = np.zeros((128, NCST), np.float32)
    idx = np.arange(128)
    c[:, C_ID:C_ID + 128] = np.eye(128, dtype=np.float32)
    c[:, C_TRI:C_TRI + 128] = (idx[:, None] <= idx[None, :]).astype(np.float32)
    c[:, C_ONE:C_ONE + 128] = 1.0
    neg = np.where(idx[:, None] > idx[None, :], -30000.0, 0.0).astype(np.float32)
    c[:, C_NEG:C_NEG + 512] = np.tile(neg, (1, 4))
    c[:, C_IOTA:C_IOTA + 512] = np.arange(512, dtype=np.float32)[None, :]
    c[:, C_IFQ] = (np.float32(10000.0) ** (-np.linspace(0.0, 1.0, 128, dtype=np.float32))).astype(np.float32)
    for h in range(8):
        lg = math.log1p(-2.0 ** (-5.0 - h))
        s = idx[:, None]
        l_ = idx[None, :]
        same = (s // 64) == (l_ // 64)
        m = np.where(same, np.where(l_ >= s, 1.0, np.exp(lg * 2.0 * (s - l_))), np.where((s // 64) < (l_ // 64), 1.0, 0.0))
        c[:, C_DFIX + 128 * h:C_DFIX + 128 * h + 128] = m.astype(np.float32)
    return c


def make_pv(inp, b):
    pvv = np.zeros((L, 128, NPV), np.float32)

    def col(v):
        return np.ascontiguousarray(v.reshape(-1, 128).T)

    for l in range(L):
        pvv[l, :, P_N1:P_N1 + 16] = col(inp["norm_ffn1_w"][l])
        pvv[l, :, P_N2:P_N2 + 16] = col(inp["norm_mix_w"][l])
        pvv[l, :, P_N3:P_N3 + 16] = col(inp["norm_ffn2_w"][l])
        pvv[l, :, P_ADAB:P_ADAB + 144] = col(inp["ada_b"][l])
        pvv[l, :, P_RNW:P_RNW + 16] = col(inp["ret_norm_w"][l])
        pvv[l, :, P_SNW:P_SNW + 16] = col(inp["ssd_norm_w"][l])
        for k in range(4):
            pvv[l, :, P_CW + 24 * k:P_CW + 24 * k + 24] = col(inp["conv_w"][l, k])
        pvv[l, :, P_CB:P_CB + 24] = col(inp["conv_b"][l])
        pvv[l, :, P_FN:P_FN + 16] = col(inp["final_norm_w"])
        pvv[l, :, P_C:P_C + 16] = col(inp["c"][b])
        pvv[l, :, P_DTB:P_DTB + 128] = np.tile(inp["dt_bias"][l], 4)[None, :]
        pvv[l, :, P_ALOG:P_ALOG + 128] = np.tile(inp["a_log"][l], 4)[None, :]
        pvv[l, :, P_DSK:P_DSK + 32] = inp["d_skip"][l][None, :]
    return pvv


def run(inputs, NT, n_cores=2):
    inp = {k: np.asarray(v) for k, v in inputs.items()}
    nc = build(NT)
    cst = make_consts()
    SC = NT * T
    maps = []
    for c in range(n_cores):
        b = c % 2
        m = {"x": np.ascontiguousarray(inp["x"][b, :SC]), "pos": np.ascontiguousarray(inp["positions"][b:b + 1, :SC]).astype(np.int32),
             "pv": make_pv(inp, b), "cst": cst}
        for k in ("ada_w", "ffn1_w1", "ffn1_w3", "ffn1_w2", "ffn2_w1", "ffn2_w3", "ffn2_w2", "w_in", "w_out"):
            m[k] = inp[k]
        maps.append(m)
    res = run_bass_kernel_spmd(nc, maps, core_ids=list(range(n_cores)))
    if "dbg" in res.results[0]:
        np.save("dbg_out.npy", res.results[0]["dbg"])
    return np.stack([res.results[b]["out"] for b in range(2)], axis=0)


def kernel(**inputs):
    return run(inputs, 16384 // T).astype(np.float32)
```

```python
import math
import os
import contextlib
import numpy as np
import concourse.bass as bass
import concourse.mybir as mybir
from concourse.bass_utils import run_bass_kernel_spmd

F32 = mybir.dt.float32
BF16 = mybir.dt.bfloat16
I32 = mybir.dt.int32
AF = mybir.ActivationFunctionType
ALU = mybir.AluOpType

D = 2048
DFF = 5632
L = 2
T = 512
EPS = 1e-6
NPV = 672
ZO, XO, BO, CO, DTO, QO, KO, VO, GO = 0, 2048, 4096, 4608, 5120, 5152, 7200, 9248, 11296
P_N1, P_N2, P_N3, P_ADAB, P_RNW, P_SNW, P_CW, P_CB, P_FN, P_C, P_DTB, P_ALOG, P_DSK = (
    0, 16, 32, 48, 192, 208, 224, 320, 344, 360, 376, 504, 632)
C_ID, C_TRI, C_ONE, C_NEG, C_IOTA, C_IFQ, C_DFIX = 0, 128, 256, 384, 896, 1408, 1409
NCST = 1409 + 8 * 128


class Tok:
    __slots__ = ("w", "r")

    def __init__(self):
        self.w = None
        self.r = []


class B:
    __slots__ = ("ap", "t")

    def __init__(self, ap, t=None):
        self.ap = ap
        self.t = t if t is not None else Tok()


class _Rec:
    def __getattr__(self, name):
        return lambda *a, **k: (name, a, k)


def _eager(fn):
    name, a, k = fn(_Rec())
    return lambda e: getattr(e, name)(*a, **k)


class Sched:
    def __init__(self, nc, stack):
        self.nc = nc
        self.stack = stack
        self.engs = ["pe", "act", "dve", "pool", "sp"]
        self.sems = {}
        self.cnt = {}
        self.unit = {}
        self.waited = {e: {} for e in self.engs}
        self.streams = {e: [] for e in self.engs}
        self.pe_chain = False
        for e in self.engs:
            self.newsem(e, 1)

    def newsem(self, name, unit):
        self.sems[name] = self.stack.enter_context(self.nc.semaphore("s_" + name))
        self.cnt[name] = 0
        self.unit[name] = unit
        return name

    def _deps(self, eng, reads, writes):
        deps = {}
        for t in reads:
            if t.w is not None:
                deps[t.w[0]] = max(deps.get(t.w[0], 0), t.w[1])
        for t in writes:
            if t.w is not None:
                deps[t.w[0]] = max(deps.get(t.w[0], 0), t.w[1])
            for (e2, c) in t.r:
                deps[e2] = max(deps.get(e2, 0), c)
        waits = []
        for e2, c in deps.items():
            if e2 == "pe" and eng == "pe" and (self.pe_chain or self.cnt["pe"] - c > 128):
                continue
            if self.unit[e2] == 16:
                c = self.cnt[e2]
            if self.waited[eng].get(e2, 0) < c:
                self.waited[eng][e2] = c
                waits.append((e2, c * self.unit[e2]))
        return waits

    def _mark(self, me, reads, writes):
        for t in reads:
            t.r = [x for x in t.r if x[0] != me[0]] + [me]
        for t in writes:
            t.w = me
            t.r = []

    def op(self, eng, fn, reads=(), writes=()):
        waits = self._deps(eng, reads, writes)
        self.cnt[eng] += 1
        self.streams[eng].append((waits, _eager(fn), eng, 1))
        self._mark((eng, self.cnt[eng]), reads, writes)

    def dma(self, q, dsem, fn, reads=(), writes=()):
        waits = self._deps(q, reads, writes)
        self.cnt[dsem] += 1
        self.streams[q].append((waits, _eager(fn), dsem, 16))
        self._mark((dsem, self.cnt[dsem]), reads, writes)

    def barrier(self):
        for e in self.engs:
            waits = []
            for n in self.sems:
                c = self.cnt[n]
                if c > 0 and self.waited[e].get(n, 0) < c:
                    self.waited[e][n] = c
                    waits.append((n, c * self.unit[n]))
            if waits:
                self.streams[e].append((waits, None, None, 0))

    def emit(self, block):
        hmap = {"pe": "tensor", "act": "scalar", "dve": "vector", "pool": "gpsimd", "sp": "sync"}
        for e in self.engs:
            stream = self.streams[e]
            if not stream:
                continue

            def body(engine, stream=stream):
                for waits, fn, semname, inc in stream:
                    for (e2, v) in waits:
                        engine.wait_ge(self.sems[e2], v)
                    if fn is not None:
                        fn(engine).then_inc(self.sems[semname], inc)

            getattr(block, hmap[e])(body)


class Scratch:
    def __init__(self, ap_bf16, n):
        self.base = ap_bf16
        self.n = n
        self.off = 0

    def reset(self):
        self.off = 0

    def alloc(self, shape, dt):
        n = int(np.prod(shape))
        nb = n * 2 if dt == F32 else n
        nb = (nb + 1) // 2 * 2
        assert self.off + nb <= self.n, (self.off, nb, self.n)
        v = self.base[:, self.off:self.off + nb]
        self.off += nb
        if dt == F32:
            v = v.bitcast(F32)
        v = v[:, 0:n]
        if len(shape) == 2:
            v = v.rearrange("p (a b) -> p a b", a=shape[0])
        elif len(shape) == 3:
            v = v.rearrange("p (a b c) -> p a b c", a=shape[0], b=shape[1])
        return B(v)


def build(NT):
    nc = bass.Bass("TRN2", target_bir_lowering=False)
    SC = NT * T

    def din(name, shape, dt=F32):
        return nc.dram_tensor(name, shape, dt, kind="ExternalInput").ap()

    x_d = din("x", [SC, D])
    pos_d = din("pos", [1, SC], I32)
    pv_d = din("pv", [L, 128, NPV])
    cst_d = din("cst", [128, NCST])
    ada_w = din("ada_w", [L, D, 9 * D])
    w1 = [din("ffn1_w1", [L, D, DFF]), din("ffn2_w1", [L, D, DFF])]
    w3 = [din("ffn1_w3", [L, D, DFF]), din("ffn2_w3", [L, D, DFF])]
    w2 = [din("ffn1_w2", [L, DFF, D]), din("ffn2_w2", [L, DFF, D])]
    w_in = din("w_in", [L, D, 13344])
    w_out = din("w_out", [L, 4096, D])
    out_d = nc.dram_tensor("out", [SC, D], F32, kind="ExternalOutput").ap()
    DBG = os.environ.get("KDBG", "full")
    dbg_d = nc.dram_tensor("dbg", [128, 4096], F32, kind="ExternalOutput").ap() if "d" in DBG else None
    ss_d = nc.dram_tensor("ss_d", [L * 4, 128, 512], F32, kind="Internal").ap()
    rs_d = nc.dram_tensor("rs_d", [L * 8, 128, 512], F32, kind="Internal").ap()

    lgam = [math.log1p(-2.0 ** (-5.0 - h)) for h in range(8)]

    with contextlib.ExitStack() as st:
        sc = Sched(nc, st)
        sb = lambda name, shape, dt: st.enter_context(nc.sbuf_tensor("sb_" + name, shape, dt))
        hT_ = sb("hT", [128, 16, T], F32)
        nT_ = sb("nT", [128, 16, T], BF16)
        SCRN = 32 * 1024
        scr_ = sb("scr", [128, SCRN], BF16)
        scr = Scratch(scr_, SCRN)
        NSLOT = 4
        wslots = [sb(f"wslot{i}", [128, 8192], BF16) for i in range(NSLOT)]
        wtok = [Tok() for _ in range(NSLOT)]
        for i in range(NSLOT):
            sc.newsem(f"dw{i}", 16)
        sc.newsem("dl", 16)
        sc.newsem("ds", 16)
        cst = sb("cst", [128, NCST], F32)
        cst_t = Tok()
        cst16 = sb("cst16", [128, 384], BF16)
        pv = sb("pv", [128, L, NPV], F32)
        pv_t = Tok()
        modv = sb("modv", [128, L, 144], F32)
        coef = sb("coef", [128, L, 96], F32)
        aneg = sb("aneg", [128, L, 128], F32)
        coef_t = Tok()
        cond16 = sb("cond16", [128, 16], BF16)
        cosb = sb("cosb", [128, T], F32)
        sinb = sb("sinb", [128, T], F32)
        cs_t = Tok()
        halo = sb("halo", [128, L, 24, 3], F32)
        halo_t = [[Tok() for _ in range(24)] for _ in range(L)]
        psb = [st.enter_context(nc.psum_tensor(f"ps{i}", [128, 512], F32)) for i in range(8)]
        pst = [Tok() for _ in range(8)]
        hT_t = [Tok() for _ in range(16)]
        nT_t = [Tok() for _ in range(16)]
        state = {"slot": 0, "bank": 2}

        ident = cst[:, C_ID:C_ID + 128]
        tri = cst[:, C_TRI:C_TRI + 128]
        ones = cst[:, C_ONE:C_ONE + 128]
        negm = cst[:, C_NEG:C_NEG + 512]
        iota = cst[:, C_IOTA:C_IOTA + 512]
        ifq = cst[:, C_IFQ:C_IFQ + 1]
        ident16 = cst16[:, 0:128]
        ones16 = cst16[:, 128:256]

        def bank():
            b = state["bank"]
            state["bank"] = 2 + (b - 2 + 1) % 6
            return B(psb[b][:, :], pst[b])

        def wslab(src_ap, kc, cols):
            i = state["slot"]
            state["slot"] = (i + 1) % NSLOT
            v = wslots[i][:, 0:kc * cols].rearrange("p (k c) -> p k c", k=kc)
            q = "pool"
            sc.dma(q, f"dw{i}", lambda e: e.dma_start(out=v, in_=src_ap), writes=[wtok[i]])
            return B(v, wtok[i])

        def PE(fn, r, w):
            sc.op("pe", fn, reads=[b.t for b in r], writes=[b.t for b in w])

        def ACT(fn, r, w):
            sc.op("act", fn, reads=[b.t for b in r], writes=[b.t for b in w])

        def DVE(fn, r, w):
            sc.op("dve", fn, reads=[b.t for b in r], writes=[b.t for b in w])

        def mm(out, lhsT, rhs, start, stop, r, w):
            sc.pe_chain = not start
            PE(lambda e: e.matmul(out, lhsT, rhs, start=start, stop=stop), r, w)
            sc.pe_chain = False

        def dump(b, ap, c0):
            if dbg_d is None:
                return
            n = ap.shape[-1]
            sc.dma("sp", "ds", lambda e: e.dma_start(out=dbg_d[:, c0:c0 + n], in_=ap), reads=[b.t])

        CST = B(cst[:, :], cst_t)
        PV = B(pv[:, :, :], pv_t)
        COEF = B(coef[:, :, :], coef_t)
        CS = B(cosb[:, :], cs_t)
        hTb = [B(hT_[:, dc, :], hT_t[dc]) for dc in range(16)]
        nTb = [B(nT_[:, dc, :], nT_t[dc]) for dc in range(16)]

        sc.dma("sp", "dl", lambda e: e.dma_start(out=cst[:, :], in_=cst_d), writes=[cst_t])
        sc.dma("sp", "dl", lambda e: e.dma_start(out=pv[:, :, :], in_=pv_d.rearrange("l p n -> p l n")), writes=[pv_t])
        DVE(lambda e: e.tensor_copy(out=cst16[:, 0:128], in_=ident), [CST], [CST])
        DVE(lambda e: e.tensor_copy(out=cst16[:, 128:256], in_=ones), [CST], [CST])
        DVE(lambda e: e.memset(halo[:, :, :, :], 0.0), [], [B(None, t) for l_ in halo_t for t in l_])
        zb = scr.alloc([512], F32)
        DVE(lambda e: e.memset(zb.ap, 0.0), [], [zb])
        for i in range(L * 4):
            sc.dma("sp", "ds", lambda e, i=i: e.dma_start(out=ss_d[i], in_=zb.ap), reads=[zb.t])
        for i in range(L * 8):
            sc.dma("sp", "ds", lambda e, i=i: e.dma_start(out=rs_d[i], in_=zb.ap), reads=[zb.t])
        ACT(lambda e: e.activation(out=cond16[:, :], in_=pv[:, 0, P_C:P_C + 16], func=AF.Silu), [PV], [COEF])
        for l in range(L):
            pb = bank()
            for s in range(36):
                W = wslab(ada_w[l][:, 512 * s:512 * s + 512].rearrange("(k p) n -> p k n", p=128), 16, 512)
                for cc in range(4):
                    j = 4 * s + cc
                    for k in range(16):
                        mm(pb.ap[:, j:j + 1], W.ap[:, k, cc * 128:cc * 128 + 128], cond16[:, k:k + 1],
                           k == 0, k == 15, [W, COEF], [pb])
            DVE(lambda e, l=l, pb=pb: e.tensor_tensor(out=modv[:, l, :], in0=pb.ap[:, 0:144], in1=pv[:, l, P_ADAB:P_ADAB + 144], op=ALU.add),
                [pb, PV], [COEF])
            for i, (pn, scj) in enumerate([(P_N1, 1), (P_N2, 4), (P_N3, 7)]):
                DVE(lambda e, l=l, i=i, pn=pn, scj=scj: e.scalar_tensor_tensor(
                    out=coef[:, l, 16 * i:16 * i + 16], in0=modv[:, l, 16 * scj:16 * scj + 16], scalar=1.0,
                    in1=pv[:, l, pn:pn + 16], op0=ALU.add, op1=ALU.mult), [COEF, PV], [COEF])
            for i, (gj, f) in enumerate([(2, 0.5), (5, 1.0), (8, 0.5)]):
                DVE(lambda e, l=l, i=i, gj=gj, f=f: e.tensor_scalar(
                    out=coef[:, l, 48 + 16 * i:48 + 16 * i + 16], in0=modv[:, l, 16 * gj:16 * gj + 16],
                    scalar1=f, scalar2=None, op0=ALU.mult), [COEF], [COEF])
            ACT(lambda e, l=l: e.activation(out=aneg[:, l, :], in_=pv[:, l, P_ALOG:P_ALOG + 128], func=AF.Exp), [PV], [COEF])
            DVE(lambda e, l=l: e.tensor_scalar(out=aneg[:, l, :], in0=aneg[:, l, :], scalar1=-1.0, scalar2=None, op0=ALU.mult),
                [COEF], [COEF])
        sc.newsem("dc", 16)

        def cache(name, src):
            Lx, R, C = src.shape
            dst = nc.dram_tensor(name + "_b16", [Lx, R, C], BF16, kind="Internal").ap()
            for l_ in range(Lx):
                for r0 in range(0, R, 128):
                    sc.dma("pool", "dc", lambda e: e.dma_start(out=dst[l_][r0:r0 + 128, :], in_=src[l_][r0:r0 + 128, :]))
            return dst

        if os.environ.get("KNOCACHE") is None:
            for wi in range(2):
                w1[wi] = cache(f"w1_{wi}", w1[wi])
                w3[wi] = cache(f"w3_{wi}", w3[wi])
                w2[wi] = cache(f"w2_{wi}", w2[wi])
            w_in = cache("w_in", w_in)
            w_out = cache("w_out", w_out)
        sc.barrier()

        def rms_rstd(srcs):
            pb = bank()
            sq = [scr.alloc([T], BF16) for _ in range(2)]
            for dc in range(16):
                s = sq[dc % 2]
                ACT(lambda e, s=s, dc=dc: e.activation(out=s.ap, in_=srcs[dc].ap, func=AF.Square), [srcs[dc]], [s])
                mm(pb.ap, ones16, s.ap, dc == 0, dc == 15, [s, CST], [pb])
            rstd = scr.alloc([T], F32)
            ACT(lambda e: e.activation(out=rstd.ap, in_=pb.ap, func=AF.Sqrt, bias=EPS, scale=1.0 / D), [pb], [rstd])
            DVE(lambda e: e.reciprocal(out=rstd.ap, in_=rstd.ap), [rstd], [rstd])
            return rstd

        def rms_mod(l, i):
            rstd = rms_rstd(hTb)
            tmp = [scr.alloc([T], F32) for _ in range(2)]
            shj = [0, 3, 6][i]
            for dc in range(16):
                t_ = tmp[dc % 2]
                DVE(lambda e, dc=dc, t_=t_: e.scalar_tensor_tensor(out=t_.ap, in0=hTb[dc].ap, scalar=coef[:, l, 16 * i + dc:16 * i + dc + 1],
                                                                   in1=rstd.ap, op0=ALU.mult, op1=ALU.mult), [hTb[dc], rstd, COEF], [t_])
                ACT(lambda e, dc=dc, t_=t_: e.activation(out=nTb[dc].ap, in_=t_.ap, func=AF.Identity,
                                                         bias=modv[:, l, 16 * shj + dc:16 * shj + dc + 1]), [t_, COEF], [nTb[dc]])

        def resid_add(l, gi, dc, pb):
            DVE(lambda e: e.scalar_tensor_tensor(out=hTb[dc].ap, in0=pb.ap, scalar=coef[:, l, 48 + 16 * gi + dc:48 + 16 * gi + dc + 1],
                                                 in1=hTb[dc].ap, op0=ALU.mult, op1=ALU.add), [pb, hTb[dc], COEF], [hTb[dc]])

        def ffn(l, which):
            sc.barrier()
            scr.reset()
            rms_mod(l, 0 if which == 0 else 2)
            act = [scr.alloc([T], BF16) for _ in range(44)]
            sa = [scr.alloc([T], F32) for _ in range(2)]
            for s in range(11):
                W1 = wslab(w1[which][l][:, 512 * s:512 * s + 512].rearrange("(k p) n -> p k n", p=128), 16, 512)
                W3 = wslab(w3[which][l][:, 512 * s:512 * s + 512].rearrange("(k p) n -> p k n", p=128), 16, 512)
                for cc in range(4):
                    fc = 4 * s + cc
                    pa, pb = bank(), bank()
                    for k in range(16):
                        mm(pa.ap, W1.ap[:, k, cc * 128:cc * 128 + 128], nTb[k].ap, k == 0, k == 15, [W1, nTb[k]], [pa])
                    for k in range(16):
                        mm(pb.ap, W3.ap[:, k, cc * 128:cc * 128 + 128], nTb[k].ap, k == 0, k == 15, [W3, nTb[k]], [pb])
                    s_ = sa[fc % 2]
                    ACT(lambda e, s_=s_, pa=pa: e.activation(out=s_.ap, in_=pa.ap, func=AF.Silu), [pa], [s_])
                    DVE(lambda e, s_=s_, pb=pb, fc=fc: e.tensor_tensor(out=act[fc].ap, in0=pb.ap, in1=s_.ap, op=ALU.mult), [pb, s_], [act[fc]])
            for dc in range(16):
                W2 = wslab(w2[which][l][:, 128 * dc:128 * dc + 128].rearrange("(k p) n -> p k n", p=128), 44, 128)
                pb = bank()
                for k in range(44):
                    mm(pb.ap, W2.ap[:, k, :], act[k].ap, k == 0, k == 43, [W2, act[k]], [pb])
                resid_add(l, 0 if which == 0 else 2, dc, pb)

        def proj_fm(l, col0, ncols_chunks, W, wc0, evac):
            for i in range(ncols_chunks):
                pb = bank()
                for k in range(16):
                    mm(pb.ap, W.ap[:, k, wc0 + i * 128:wc0 + i * 128 + 128], nTb[k].ap, k == 0, k == 15, [W, nTb[k]], [pb])
                evac(pb, i)

        def win_slab(l, c0, ncols):
            return wslab(w_in[l][:, c0:c0 + ncols].rearrange("(k p) n -> p k n", p=128), 16, ncols)

        def ssd(l):
            sc.barrier()
            scr.reset()
            rms_mod(l, 1)
            A = scr.alloc
            Wdt = win_slab(l, DTO, 32)
            pb = bank()
            for blk in range(4):
                for k in range(16):
                    mm(pb.ap[:, blk * 32:blk * 32 + 32], nT_[:, k, blk * 128:blk * 128 + 128], Wdt.ap[:, k, :], k == 0, k == 15, [Wdt, nTb[k]], [pb])
            xd = A([128], F32)
            ax = A([128], F32)
            dt = A([128], F32)
            dA = A([128], F32)
            acum = A([128], F32)
            nacum = A([128], F32)
            atot = A([128], F32)
            decin = A([128], F32)
            dtdo = A([128], F32)
            cdec = A([128], F32)
            if dbg_d is not None:
                wd32 = A([512], F32)
                DVE(lambda e: e.tensor_copy(out=wd32.ap, in_=Wdt.ap.rearrange("p k c -> p (k c)")), [Wdt], [wd32])
                dump(wd32, wd32.ap, 2560)
                n32 = A([512], F32)
                DVE(lambda e: e.tensor_copy(out=n32.ap, in_=nT_[:, 0, :]), [nTb[0]], [n32])
                dump(n32, n32.ap, 3072)
                dr = A([128], F32)
                DVE(lambda e: e.tensor_copy(out=dr.ap, in_=pb.ap[:, 0:128]), [pb], [dr])
                dump(dr, dr.ap, 3584)
            DVE(lambda e: e.tensor_tensor(out=xd.ap, in0=pb.ap[:, 0:128], in1=pv[:, l, P_DTB:P_DTB + 128], op=ALU.add), [pb, PV], [xd])
            DVE(lambda e: e.tensor_scalar(out=ax.ap, in0=xd.ap, scalar1=-1.0, scalar2=None, op0=ALU.mult), [xd], [ax])
            DVE(lambda e: e.tensor_tensor(out=ax.ap, in0=ax.ap, in1=xd.ap, op=ALU.max), [ax, xd], [ax])
            ACT(lambda e: e.activation(out=ax.ap, in_=ax.ap, func=AF.Exp, scale=-1.0), [ax], [ax])
            ACT(lambda e: e.activation(out=ax.ap, in_=ax.ap, func=AF.Ln, bias=1.0), [ax], [ax])
            DVE(lambda e: e.scalar_tensor_tensor(out=dt.ap, in0=xd.ap, scalar=0.0, in1=ax.ap, op0=ALU.max, op1=ALU.add), [xd, ax], [dt])
            DVE(lambda e: e.tensor_tensor(out=dA.ap, in0=dt.ap, in1=aneg[:, l, :], op=ALU.mult), [dt, COEF], [dA])
            pc = bank()
            for blk in range(4):
                mm(pc.ap[:, blk * 32:blk * 32 + 32], tri, dA.ap[:, blk * 32:blk * 32 + 32], True, True, [dA, CST], [pc])
            mm(pc.ap[:, 128:256], ones, dA.ap, True, True, [dA, CST], [pc])
            DVE(lambda e: e.tensor_copy(out=acum.ap, in_=pc.ap[:, 0:128]), [pc], [acum])
            DVE(lambda e: e.tensor_scalar(out=nacum.ap, in0=pc.ap[:, 0:128], scalar1=-1.0, scalar2=None, op0=ALU.mult), [pc], [nacum])
            DVE(lambda e: e.tensor_copy(out=atot.ap, in_=pc.ap[:, 128:256]), [pc], [atot])
            ACT(lambda e: e.activation(out=decin.ap, in_=acum.ap, func=AF.Exp), [acum], [decin])
            DVE(lambda e: e.tensor_tensor(out=dtdo.ap, in0=atot.ap, in1=acum.ap, op=ALU.subtract), [atot, acum], [dtdo])
            ACT(lambda e: e.activation(out=dtdo.ap, in_=dtdo.ap, func=AF.Exp), [dtdo], [dtdo])
            DVE(lambda e: e.tensor_tensor(out=dtdo.ap, in0=dtdo.ap, in1=dt.ap, op=ALU.mult), [dtdo, dt], [dtdo])
            ACT(lambda e: e.activation(out=cdec.ap, in_=atot.ap, func=AF.Exp), [atot], [cdec])

            for i_, b_ in enumerate([xd, dt, dA, acum, atot, decin, dtdo, cdec]):
                dump(b_, b_.ap, 128 * i_)
            ubuf = A([T + 4], F32)
            acc = A([T], F32)
            xT = A([4, T], BF16)
            BT = A([T], BF16)
            CT = A([T], BF16)
            xtok = A([4, 512], BF16)
            btok = A([4, 128], BF16)
            zs = A([4, 512], BF16)
            S = A([512], F32)
            S16 = A([512], BF16)
            cbT = A([128], F32)
            R = A([8, 128], F32)
            Lm = A([8, 128], F32)
            M = A([8, 128], BF16)
            xdt = A([512], BF16)
            xdd = A([512], BF16)
            y1 = A([512], F32)
            y2 = A([512], F32)
            ssq = A([2], F32)
            yn = A([512], BF16)
            yT = A([4, T], BF16)
            junk = A([512], BF16)

            def b3(ap2, h0):
                return ap2.unsqueeze(2).broadcast_to([128, 8, 64])

            def v3(ap2):
                return ap2.rearrange("p (h q) -> p h q", h=8)

            for g in range(4):
                Wx = win_slab(l, XO + 512 * g, 512)
                WB = win_slab(l, BO + 128 * g, 128)
                WC = win_slab(l, CO + 128 * g, 128)
                jobs = [(Wx, i * 128, 4 * g + i, xT.ap[:, i, :]) for i in range(4)] + [(WB, 0, 16 + g, BT.ap), (WC, 0, 20 + g, CT.ap)]
                for (W, wc0, ci, dst) in jobs:
                    dstB = xT if W is Wx else (BT if W is WB else CT)
                    HB = B(None, halo_t[l][ci])
                    pb = bank()
                    for k in range(16):
                        mm(pb.ap, W.ap[:, k, wc0:wc0 + 128], nTb[k].ap, k == 0, k == 15, [W, nTb[k]], [pb])
                    ACT(lambda e, pb=pb: e.activation(out=ubuf.ap[:, 3:3 + T], in_=pb.ap, func=AF.Identity), [pb], [ubuf])
                    DVE(lambda e, ci=ci: e.tensor_copy(out=ubuf.ap[:, 0:3], in_=halo[:, l, ci, :]), [HB], [ubuf])
                    DVE(lambda e, ci=ci: e.tensor_copy(out=halo[:, l, ci, :], in_=ubuf.ap[:, T:T + 3]), [ubuf], [HB])
                    cw = lambda k_, ci=ci: pv[:, l, P_CW + k_ * 24 + ci:P_CW + k_ * 24 + ci + 1]
                    DVE(lambda e, ci=ci, cw=cw: e.tensor_scalar(out=acc.ap, in0=ubuf.ap[:, 0:T], scalar1=cw(0), scalar2=pv[:, l, P_CB + ci:P_CB + ci + 1],
                                                                op0=ALU.mult, op1=ALU.add), [ubuf, PV], [acc])
                    for k_ in range(1, 4):
                        DVE(lambda e, k_=k_, cw=cw: e.scalar_tensor_tensor(out=acc.ap, in0=ubuf.ap[:, k_:k_ + T], scalar=cw(k_), in1=acc.ap,
                                                                           op0=ALU.mult, op1=ALU.add), [ubuf, acc, PV], [acc])
                    ACT(lambda e, dst=dst: e.activation(out=dst, in_=acc.ap, func=AF.Silu), [acc], [dstB])
                Wz = win_slab(l, ZO + 512 * g, 512)
                for blk in range(4):
                    pb = bank()
                    for k in range(16):
                        mm(pb.ap, nT_[:, k, blk * 128:blk * 128 + 128], Wz.ap[:, k, :], k == 0, k == 15, [Wz, nTb[k]], [pb])
                    ACT(lambda e, pb=pb, blk=blk: e.activation(out=zs.ap[:, blk, :], in_=pb.ap, func=AF.Silu), [pb], [zs])
                for blk in range(4):
                    pb = bank()
                    pbv = pb.ap.bitcast(BF16)
                    for i in range(4):
                        PE(lambda e, pbv=pbv, i=i, blk=blk: e.transpose(pbv[:, i * 128:i * 128 + 128], xT.ap[:, i, blk * 128:blk * 128 + 128], ident16),
                           [xT, CST], [pb])
                    PE(lambda e, pbv=pbv, blk=blk: e.transpose(pbv[:, 512:640], BT.ap[:, blk * 128:blk * 128 + 128], ident16), [BT, CST], [pb])
                    DVE(lambda e, pbv=pbv, blk=blk: e.tensor_copy(out=xtok.ap[:, blk, :], in_=pbv[:, 0:512]), [pb], [xtok])
                    DVE(lambda e, pbv=pbv, blk=blk: e.tensor_copy(out=btok.ap[:, blk, :], in_=pbv[:, 512:640]), [pb], [btok])
                sc.dma("sp", "dl", lambda e, g=g: e.dma_start(out=S.ap, in_=ss_d[l * 4 + g]), writes=[S.t])
                DVE(lambda e: e.tensor_copy(out=S16.ap, in_=S.ap), [S], [S16])
                for blk in range(4):
                    bs = slice(blk * 128, blk * 128 + 128)
                    hs = slice(blk * 32 + 8 * g, blk * 32 + 8 * g + 8)
                    pcb = bank()
                    mm(pcb.ap[:, 0:128], BT.ap[:, bs], CT.ap[:, bs], True, True, [BT, CT], [pcb])
                    DVE(lambda e, pcb=pcb: e.tensor_copy(out=cbT.ap, in_=pcb.ap[:, 0:128]), [pcb], [cbT])
                    DVE(lambda e, hs=hs: e.tensor_tensor(out=R.ap, in0=dA.ap[:, hs].unsqueeze(2).broadcast_to([128, 8, 128]),
                                                         in1=tri.unsqueeze(1).broadcast_to([128, 8, 128]), op=ALU.mult), [dA, CST], [R])
                    for hb in range(2):
                        pseg = bank()
                        mm(pseg.ap, ones, R.ap[:, 4 * hb:4 * hb + 4, :], True, False, [R, CST], [pseg])
                        mm(pseg.ap, ident, negm, False, True, [CST], [pseg])
                        for hh in range(4):
                            h = 4 * hb + hh
                            ACT(lambda e, pseg=pseg, hh=hh, h=h, blk=blk: e.activation(
                                out=Lm.ap[:, h, :], in_=pseg.ap[:, hh * 128:hh * 128 + 128], func=AF.Exp,
                                bias=nacum.ap[:, blk * 32 + 8 * g + h:blk * 32 + 8 * g + h + 1]), [pseg, nacum], [Lm])
                    DVE(lambda e: e.tensor_tensor(out=M.ap, in0=Lm.ap, in1=cbT.ap.unsqueeze(1).broadcast_to([128, 8, 128]), op=ALU.mult), [Lm, cbT], [M])
                    DVE(lambda e, blk=blk, hs=hs: e.tensor_tensor(out=v3(xdt.ap), in0=v3(xtok.ap[:, blk, :]), in1=b3(dt.ap[:, hs], 0), op=ALU.mult),
                        [xtok, dt], [xdt])
                    DVE(lambda e, blk=blk, hs=hs: e.tensor_tensor(out=v3(xdd.ap), in0=v3(xtok.ap[:, blk, :]), in1=b3(dtdo.ap[:, hs], 0), op=ALU.mult),
                        [xtok, dtdo], [xdd])
                    py = bank()
                    for h in range(8):
                        mm(py.ap[:, h * 64:h * 64 + 64], M.ap[:, h, :], xdt.ap[:, h * 64:h * 64 + 64], True, True, [M, xdt], [py])
                    po = bank()
                    mm(po.ap, CT.ap[:, bs], S16.ap, True, True, [CT, S16], [po])
                    DVE(lambda e, po=po, hs=hs: e.tensor_tensor(out=v3(y1.ap), in0=v3(po.ap), in1=b3(decin.ap[:, hs], 0), op=ALU.mult), [po, decin], [y1])
                    DVE(lambda e, py=py: e.tensor_tensor(out=y2.ap, in0=py.ap, in1=y1.ap, op=ALU.add), [py, y1], [y2])
                    DVE(lambda e, blk=blk: e.tensor_tensor(out=v3(y1.ap), in0=v3(xtok.ap[:, blk, :]),
                                                           in1=b3(pv[:, l, P_DSK + 8 * g:P_DSK + 8 * g + 8], 0), op=ALU.mult), [xtok, PV], [y1])
                    DVE(lambda e: e.tensor_tensor(out=y2.ap, in0=y2.ap, in1=y1.ap, op=ALU.add), [y2, y1], [y2])
                    DVE(lambda e, blk=blk: e.tensor_tensor(out=y2.ap, in0=y2.ap, in1=zs.ap[:, blk, :], op=ALU.mult), [y2, zs], [y2])
                    if g == 0 and blk == 0:
                        dump(cbT, cbT.ap, 1024)
                        dump(Lm, Lm.ap[:, 0, :], 1152)
                        dump(y2, y2.ap, 1280)
                    DVE(lambda e: e.memset(ssq.ap, 0.0), [], [ssq])
                    ACT(lambda e: e.activation(out=junk.ap, in_=y2.ap, func=AF.Square, accum_out=ssq.ap[:, 0:1]), [y2], [junk, ssq])
                    ACT(lambda e: e.activation(out=ssq.ap[:, 1:2], in_=ssq.ap[:, 0:1], func=AF.Sqrt, bias=EPS, scale=1.0 / 512), [ssq], [ssq])
                    DVE(lambda e: e.reciprocal(out=ssq.ap[:, 1:2], in_=ssq.ap[:, 1:2]), [ssq], [ssq])
                    DVE(lambda e: e.tensor_scalar(out=yn.ap, in0=y2.ap, scalar1=ssq.ap[:, 1:2], scalar2=None, op0=ALU.mult), [y2, ssq], [yn])
                    if g == 0 and blk == 0:
                        dump(ssq, ssq.ap, 1792)
                        dump(S, S.ap, 2048)
                    pt = bank()
                    ptv = pt.ap.bitcast(BF16)
                    for i in range(4):
                        PE(lambda e, ptv=ptv, i=i: e.transpose(ptv[:, i * 128:i * 128 + 128], yn.ap[:, i * 128:i * 128 + 128], ident16), [yn, CST], [pt])
                    for i in range(4):
                        ACT(lambda e, ptv=ptv, i=i, bs=bs: e.activation(out=yT.ap[:, i, bs], in_=ptv[:, i * 128:i * 128 + 128], func=AF.Identity,
                                                                       scale=pv[:, l, P_SNW + 4 * g + i:P_SNW + 4 * g + i + 1]), [pt, PV], [yT])
                    pS = bank()
                    mm(pS.ap, btok.ap[:, blk, :], xdd.ap, True, True, [btok, xdd], [pS])
                    DVE(lambda e, hs=hs: e.tensor_tensor(out=v3(S.ap), in0=v3(S.ap), in1=b3(cdec.ap[:, hs], 0), op=ALU.mult), [S, cdec], [S])
                    DVE(lambda e, pS=pS: e.tensor_tensor(out=S.ap, in0=S.ap, in1=pS.ap, op=ALU.add), [S, pS], [S])
                    DVE(lambda e: e.tensor_copy(out=S16.ap, in_=S.ap), [S], [S16])
                sc.dma("sp", "ds", lambda e, g=g: e.dma_start(out=ss_d[l * 4 + g], in_=S.ap), reads=[S.t])
                Wo = wslab(w_out[l][512 * g:512 * g + 512, :].rearrange("(k p) n -> p k n", p=128), 4, 2048)
                for dc in range(16):
                    pb = bank()
                    for i in range(4):
                        mm(pb.ap, Wo.ap[:, i, dc * 128:dc * 128 + 128], yT.ap[:, i, :], i == 0, i == 3, [Wo, yT], [pb])
                    resid_add(l, 1, dc, pb)

        def ret(l):
            sc.barrier()
            scr.reset()
            A = scr.alloc
            qraw = A([2, T], F32)
            t1 = A([T], F32)
            t2 = A([T], F32)
            cq = A([T], F32)
            sq_ = A([T], F32)
            qs = A([2, T], BF16)
            ks = A([2, T], BF16)
            vtok = A([4, 256], BF16)
            kstok = A([4, 256], BF16)
            gs = A([2, T], BF16)
            S = A([2, 256], F32)
            Sg = A([2, 256], BF16)
            stmp = A([256], F32)
            PT = [A([T], BF16) for _ in range(2)]
            sqb = A([T], BF16)
            rstd = A([T], F32)
            yfin = A([2, T], BF16)
            yps = [B(psb[0][:, :], pst[0]), B(psb[1][:, :], pst[1])]
            for hh in range(8):
                lg = lgam[hh]
                dfix = cst[:, C_DFIX + 128 * hh:C_DFIX + 128 * hh + 128]
                for which, off, dst in ((0, QO, qs), (1, KO, ks)):
                    W = win_slab(l, off + 256 * hh, 256)
                    proj_fm(l, 0, 2, W, 0, lambda pb, i: ACT(lambda e, pb=pb, i=i: e.activation(out=qraw.ap[:, i, :], in_=pb.ap, func=AF.Identity), [pb], [qraw]))
                    if which == 0:
                        ACT(lambda e, lg=lg: e.activation(out=t1.ap, in_=iota, func=AF.Exp, scale=lg), [CST], [t1])
                    else:
                        ACT(lambda e, lg=lg: e.activation(out=t1.ap, in_=iota, func=AF.Exp, scale=-lg, bias=math.log(1.0 / 16.0)), [CST], [t1])
                    DVE(lambda e: e.tensor_tensor(out=cq.ap, in0=cosb[:, :], in1=t1.ap, op=ALU.mult), [CS, t1], [cq])
                    DVE(lambda e: e.tensor_tensor(out=sq_.ap, in0=sinb[:, :], in1=t1.ap, op=ALU.mult), [CS, t1], [sq_])
                    DVE(lambda e: e.tensor_tensor(out=t1.ap, in0=qraw.ap[:, 0, :], in1=cq.ap, op=ALU.mult), [qraw, cq], [t1])
                    DVE(lambda e: e.tensor_tensor(out=t2.ap, in0=qraw.ap[:, 1, :], in1=sq_.ap, op=ALU.mult), [qraw, sq_], [t2])
                    DVE(lambda e, dst=dst: e.tensor_tensor(out=dst.ap[:, 0, :], in0=t1.ap, in1=t2.ap, op=ALU.subtract), [t1, t2], [dst])
                    DVE(lambda e: e.tensor_tensor(out=t1.ap, in0=qraw.ap[:, 0, :], in1=sq_.ap, op=ALU.mult), [qraw, sq_], [t1])
                    DVE(lambda e: e.tensor_tensor(out=t2.ap, in0=qraw.ap[:, 1, :], in1=cq.ap, op=ALU.mult), [qraw, cq], [t2])
                    DVE(lambda e, dst=dst: e.tensor_tensor(out=dst.ap[:, 1, :], in0=t1.ap, in1=t2.ap, op=ALU.add), [t1, t2], [dst])
                Wv = win_slab(l, VO + 256 * hh, 256)
                for bp in range(2):
                    pb = bank()
                    for b2 in range(2):
                        blk = 2 * bp + b2
                        for k in range(16):
                            mm(pb.ap[:, b2 * 256:b2 * 256 + 256], nT_[:, k, blk * 128:blk * 128 + 128], Wv.ap[:, k, :], k == 0, k == 15, [Wv, nTb[k]], [pb])
                    ACT(lambda e, pb=pb, bp=bp: e.activation(out=vtok.ap[:, 2 * bp:2 * bp + 2, :], in_=pb.ap.rearrange("p (a b) -> p a b", a=2), func=AF.Identity),
                        [pb], [vtok])
                Wg = win_slab(l, GO + 256 * hh, 256)
                proj_fm(l, 0, 2, Wg, 0, lambda pb, i: ACT(lambda e, pb=pb, i=i: e.activation(out=gs.ap[:, i, :], in_=pb.ap, func=AF.Silu), [pb], [gs]))
                for bp in range(2):
                    pb = bank()
                    pbv = pb.ap.bitcast(BF16)
                    for b2 in range(2):
                        blk = 2 * bp + b2
                        for dh in range(2):
                            PE(lambda e, pbv=pbv, b2=b2, dh=dh, blk=blk: e.transpose(pbv[:, b2 * 256 + dh * 128:b2 * 256 + dh * 128 + 128],
                                                                                       ks.ap[:, dh, blk * 128:blk * 128 + 128], ident16), [ks, CST], [pb])
                    DVE(lambda e, pbv=pbv, bp=bp: e.tensor_copy(out=kstok.ap[:, 2 * bp:2 * bp + 2, :], in_=pbv[:, 0:512].rearrange("p (a b) -> p a b", a=2)),
                        [pb], [kstok])
                sc.dma("sp", "dl", lambda e, hh=hh: e.dma_start(out=S.ap, in_=rs_d[l * 8 + hh].rearrange("p (a b) -> p a b", a=2)), writes=[S.t])
                DVE(lambda e, lg=lg: e.tensor_scalar(out=Sg.ap, in0=S.ap, scalar1=math.exp(lg), scalar2=None, op0=ALU.mult), [S], [Sg])
                for eh in range(2):
                    for dh in range(2):
                        mm(yps[eh].ap, Sg.ap[:, dh, eh * 128:eh * 128 + 128], qs.ap[:, dh, :], dh == 0, False, [Sg, qs], [yps[eh]])
                for sb_ in range(4):
                    nl = T - 128 * sb_
                    sps = bank()
                    for dh in range(2):
                        mm(sps.ap[:, 0:nl], ks.ap[:, dh, sb_ * 128:sb_ * 128 + 128], qs.ap[:, dh, 128 * sb_:T], dh == 0, dh == 1, [ks, qs], [sps])
                    P_ = PT[sb_ % 2]
                    DVE(lambda e, sps=sps, P_=P_, dfix=dfix: e.tensor_tensor(out=P_.ap[:, 0:128], in0=sps.ap[:, 0:128], in1=dfix, op=ALU.mult), [sps, CST], [P_])
                    if nl > 128:
                        ACT(lambda e, sps=sps, P_=P_, nl=nl: e.activation(out=P_.ap[:, 128:nl], in_=sps.ap[:, 128:nl], func=AF.Identity), [sps], [P_])
                    for eh in range(2):
                        mm(yps[eh].ap[:, 128 * sb_:T], vtok.ap[:, sb_, eh * 128:eh * 128 + 128], P_.ap[:, 0:nl], False, sb_ == 3, [vtok, P_], [yps[eh]])
                for dh in range(2):
                    pS = bank()
                    for blk in range(4):
                        mm(pS.ap[:, 0:256], kstok.ap[:, blk, dh * 128:dh * 128 + 128], vtok.ap[:, blk, :], blk == 0, blk == 3, [kstok, vtok], [pS])
                    ACT(lambda e, pS=pS, lg=lg: e.activation(out=stmp.ap, in_=pS.ap[:, 0:256], func=AF.Identity, scale=math.exp(lg * (T - 1))), [pS], [stmp])
                    DVE(lambda e, dh=dh, lg=lg: e.scalar_tensor_tensor(out=S.ap[:, dh, :], in0=S.ap[:, dh, :], scalar=math.exp(lg * T), in1=stmp.ap,
                                                                       op0=ALU.mult, op1=ALU.add), [S, stmp], [S])
                sc.dma("sp", "ds", lambda e, hh=hh: e.dma_start(out=rs_d[l * 8 + hh].rearrange("p (a b) -> p a b", a=2), in_=S.ap), reads=[S.t])
                pm = bank()
                for eh in range(2):
                    ACT(lambda e, eh=eh: e.activation(out=sqb.ap, in_=yps[eh].ap, func=AF.Square), [yps[eh]], [sqb])
                    mm(pm.ap, ones16, sqb.ap, eh == 0, eh == 1, [sqb, CST], [pm])
                ACT(lambda e, pm=pm: e.activation(out=rstd.ap, in_=pm.ap, func=AF.Sqrt, bias=EPS, scale=1.0 / 256), [pm], [rstd])
                DVE(lambda e: e.reciprocal(out=rstd.ap, in_=rstd.ap), [rstd], [rstd])
                for eh in range(2):
                    DVE(lambda e, eh=eh: e.tensor_tensor(out=t1.ap, in0=yps[eh].ap, in1=rstd.ap, op=ALU.mult), [yps[eh], rstd], [t1])
                    DVE(lambda e, eh=eh, hh=hh: e.scalar_tensor_tensor(out=yfin.ap[:, eh, :], in0=t1.ap, scalar=pv[:, l, P_RNW + 2 * hh + eh:P_RNW + 2 * hh + eh + 1],
                                                                       in1=gs.ap[:, eh, :], op0=ALU.mult, op1=ALU.mult), [t1, gs, PV], [yfin])
                Wo = wslab(w_out[l][2048 + 256 * hh:2048 + 256 * hh + 256, :].rearrange("(k p) n -> p k n", p=128), 2, 2048)
                for dc in range(16):
                    pb = bank()
                    for i in range(2):
                        mm(pb.ap, Wo.ap[:, i, dc * 128:dc * 128 + 128], yfin.ap[:, i, :], i == 0, i == 1, [Wo, yfin], [pb])
                    resid_add(l, 1, dc, pb)

        for ti in range(NT):
            sc.barrier()
            scr.reset()
            t0 = ti * T
            xt = [scr.alloc([D], F32) for _ in range(2)]
            for blk in range(4):
                xb = xt[blk % 2]
                sc.dma("sp", "dl", lambda e, xb=xb, blk=blk: e.dma_start(out=xb.ap, in_=x_d[t0 + blk * 128:t0 + blk * 128 + 128, :]), writes=[xb.t])
                for d4 in range(4):
                    pb = bank()
                    for i in range(4):
                        dc = 4 * d4 + i
                        mm(pb.ap[:, i * 128:i * 128 + 128], xb.ap[:, dc * 128:dc * 128 + 128], ident, True, True, [xb, CST], [pb])
                    ACT(lambda e, pb=pb, d4=d4, blk=blk: e.activation(out=hT_[:, 4 * d4:4 * d4 + 4, blk * 128:blk * 128 + 128],
                                                                      in_=pb.ap.rearrange("p (a b) -> p a b", a=4), func=AF.Identity),
                        [pb], [hTb[4 * d4 + i] for i in range(4)])
            pi_ = scr.alloc([T], I32) if False else None
            posi = scr.alloc([T], F32)
            posv = posi.ap.bitcast(I32)
            ang = scr.alloc([T], F32)
            r_ = scr.alloc([T], F32)
            sc.dma("sp", "dl", lambda e: e.dma_start(out=posv, in_=pos_d[0:1, t0:t0 + T].partition_broadcast(128)), writes=[posi.t])
            DVE(lambda e: e.tensor_copy(out=ang.ap, in_=posv), [posi], [ang])
            DVE(lambda e: e.tensor_scalar(out=ang.ap, in0=ang.ap, scalar1=ifq, scalar2=None, op0=ALU.mult), [ang, CST], [ang])
            kf = scr.alloc([T], F32)
            ki = kf.ap.bitcast(I32)
            m_ = scr.alloc([T], F32)
            C1 = 6.28125
            C2 = 2 * math.pi - C1
            DVE(lambda e: e.tensor_scalar(out=m_.ap, in0=ang.ap, scalar1=1.0 / (2 * math.pi), scalar2=None, op0=ALU.mult), [ang], [m_])
            DVE(lambda e: e.tensor_copy(out=ki, in_=m_.ap), [m_], [kf])
            DVE(lambda e: e.tensor_copy(out=m_.ap, in_=ki), [kf], [m_])
            DVE(lambda e: e.scalar_tensor_tensor(out=r_.ap, in0=m_.ap, scalar=-C1, in1=ang.ap, op0=ALU.mult, op1=ALU.add), [m_, ang], [r_])
            DVE(lambda e: e.scalar_tensor_tensor(out=r_.ap, in0=m_.ap, scalar=-C2, in1=r_.ap, op0=ALU.mult, op1=ALU.add), [m_, r_], [r_])

            def wrap(rb):
                DVE(lambda e: e.tensor_scalar(out=m_.ap, in0=rb.ap, scalar1=math.pi, scalar2=-2 * math.pi, op0=ALU.is_gt, op1=ALU.mult), [rb], [m_])
                DVE(lambda e: e.tensor_tensor(out=rb.ap, in0=rb.ap, in1=m_.ap, op=ALU.add), [rb, m_], [rb])
                DVE(lambda e: e.tensor_scalar(out=m_.ap, in0=rb.ap, scalar1=-math.pi, scalar2=2 * math.pi, op0=ALU.is_lt, op1=ALU.mult), [rb], [m_])
                DVE(lambda e: e.tensor_tensor(out=rb.ap, in0=rb.ap, in1=m_.ap, op=ALU.add), [rb, m_], [rb])
                DVE(lambda e: e.tensor_scalar(out=rb.ap, in0=rb.ap, scalar1=math.pi, scalar2=-math.pi, op0=ALU.min, op1=ALU.max), [rb], [rb])

            wrap(r_)
            ACT(lambda e: e.activation(out=sinb[:, :], in_=r_.ap, func=AF.Sin), [r_], [CS])
            DVE(lambda e: e.tensor_scalar(out=r_.ap, in0=r_.ap, scalar1=math.pi / 2, scalar2=None, op0=ALU.add), [r_], [r_])
            wrap(r_)
            ACT(lambda e: e.activation(out=cosb[:, :], in_=r_.ap, func=AF.Sin), [r_], [CS])
            dbg = os.environ.get("KDBG", "full")
            for l in range(L):
                if dbg == "full" or (l == 0 and "a" in dbg):
                    ffn(l, 0)
                if dbg == "full" or (l == 0 and "s" in dbg):
                    ssd(l)
                if dbg == "full" or (l == 0 and "r" in dbg):
                    if not (dbg == "full" or "s" in dbg):
                        sc.barrier()
                        scr.reset()
                        rms_mod(l, 1)
                    ret(l)
                if dbg == "full" or (l == 0 and "b" in dbg):
                    ffn(l, 1)
            sc.barrier()
            scr.reset()
            rstd = rms_rstd(hTb)
            o = [scr.alloc([T], F32) for _ in range(16)]
            for dc in range(16):
                DVE(lambda e, dc=dc: e.scalar_tensor_tensor(out=o[dc].ap, in0=hTb[dc].ap, scalar=pv[:, 0, P_FN + dc:P_FN + dc + 1], in1=rstd.ap,
                                                            op0=ALU.mult, op1=ALU.mult), [hTb[dc], rstd, PV], [o[dc]])
            ot = [scr.alloc([D], F32) for _ in range(2)]
            for blk in range(4):
                ob = ot[blk % 2]
                for d4 in range(4):
                    pb = bank()
                    for i in range(4):
                        dc = 4 * d4 + i
                        mm(pb.ap[:, i * 128:i * 128 + 128], o[dc].ap[:, blk * 128:blk * 128 + 128], ident, True, True, [o[dc], CST], [pb])
                    ACT(lambda e, pb=pb, ob=ob, d4=d4: e.activation(out=ob.ap[:, d4 * 512:d4 * 512 + 512], in_=pb.ap, func=AF.Identity), [pb], [ob])
                sc.dma("sp", "ds", lambda e, ob=ob, blk=blk: e.dma_start(out=out_d[t0 + blk * 128:t0 + blk * 128 + 128, :], in_=ob.ap), reads=[ob.t])
        sc.barrier()
        with nc.Block() as block:
            sc.emit(block)
    return nc


def make_consts():
    c = np.zeros((128, NCST), np.float32)
    idx = np.arange(128)
    c[:, C_ID:C_ID + 128] = np.eye(128, dtype=np.float32)
    c[:, C_TRI:C_TRI + 128] = (idx[:, None] <= idx[None, :]).astype(np.float32)
    c[:, C_ONE:C_ONE + 128] = 1.0
    neg = np.where(idx[:, None] > idx[None, :], -30000.0, 0.0).astype(np.float32)
    c[:, C_NEG:C_NEG + 512] = np.tile(neg, (1, 4))
    c[:, C_IOTA:C_IOTA + 512] = np.arange(512, dtype=np.float32)[None, :]
    c[:, C_IFQ] = (np.float32(10000.0) ** (-np.linspace(0.0, 1.0, 128, dtype=np.float32))).astype(np.float32)
    for h in range(8):
        lg = math.log1p(-2.0 ** (-5.0 - h))
        s = idx[:, None]
        l_ = idx[None, :]
        same = (s // 64) == (l_ // 64)
        m = np.where(same, np.where(l_ >= s, 1.0, np.exp(lg * 2.0 * (s - l_))), np.where((s // 64) < (l_ // 64), 1.0, 0.0))
        c[:, C_DFIX + 128 * h:C_DFIX + 128 * h + 128] = m.astype(np.float32)
    return c


def make_pv(inp, b):
    pvv = np.zeros((L, 128, NPV), np.float32)

    def col(v):
        return np.ascontiguousarray(v.reshape(-1, 128).T)

    for l in range(L):
        pvv[l, :, P_N1:P_N1 + 16] = col(inp["norm_ffn1_w"][l])
        pvv[l, :, P_N2:P_N2 + 16] = col(inp["norm_mix_w"][l])
        pvv[l, :, P_N3:P_N3 + 16] = col(inp["norm_ffn2_w"][l])
        pvv[l, :, P_ADAB:P_ADAB + 144] = col(inp["ada_b"][l])
        pvv[l, :, P_RNW:P_RNW + 16] = col(inp["ret_norm_w"][l])
        pvv[l, :, P_SNW:P_SNW + 16] = col(inp["ssd_norm_w"][l])
        for k in range(4):
            pvv[l, :, P_CW + 24 * k:P_CW + 24 * k + 24] = col(inp["conv_w"][l, k])
        pvv[l, :, P_CB:P_CB + 24] = col(inp["conv_b"][l])
        pvv[l, :, P_FN:P_FN + 16] = col(inp["final_norm_w"])
        pvv[l, :, P_C:P_C + 16] = col(inp["c"][b])
        pvv[l, :, P_DTB:P_DTB + 128] = np.tile(inp["dt_bias"][l], 4)[None, :]
        pvv[l, :, P_ALOG:P_ALOG + 128] = np.tile(inp["a_log"][l], 4)[None, :]
        pvv[l, :, P_DSK:P_DSK + 32] = inp["d_skip"][l][None, :]
    return pvv


def run(inputs, NT, n_cores=2):
    inp = {k: np.asarray(v) for k, v in inputs.items()}
    nc = build(NT)
    cst = make_consts()
    SC = NT * T
    maps = []
    for c in range(n_cores):
        b = c % 2
        m = {"x": np.ascontiguousarray(inp["x"][b, :SC]), "pos": np.ascontiguousarray(inp["positions"][b:b + 1, :SC]).astype(np.int32),
             "pv": make_pv(inp, b), "cst": cst}
        for k in ("ada_w", "ffn1_w1", "ffn1_w3", "ffn1_w2", "ffn2_w1", "ffn2_w3", "ffn2_w2", "w_in", "w_out"):
            m[k] = inp[k]
        maps.append(m)
    res = run_bass_kernel_spmd(nc, maps, core_ids=list(range(n_cores)))
    if "dbg" in res.results[0]:
        np.save("dbg_out.npy", res.results[0]["dbg"])
    return np.stack([res.results[b]["out"] for b in range(2)], axis=0)


def kernel(**inputs):
    return run(inputs, 16384 // T).astype(np.float32)
```
